# Optimizing a Trainium2 kernel written in Bass

```python
import math
import jax, jax.numpy as jnp
from jax import lax
import numpy as np

D_MODEL = 1024
BATCH = 2
SEQ = 16384
DEPTH = 2

GRID_W = 64
CTX_LEN = 256
N_EVEN = (DEPTH + 1) // 2
N_ODD = DEPTH // 2
D_FF = 2816
N_MOD = 9
EPS = 1e-6
NEG_INF = -1e30

GM_CHUNK = 128
GM_GROUPS = 4
GM_WIDTH = D_MODEL // 2
GM_GROUP_CH = GM_WIDTH // GM_GROUPS

HEAD_DIM = 64
N_Q_HEADS = (D_MODEL // 2) // HEAD_DIM
N_KV_HEADS = 2
Q_PER_KV = N_Q_HEADS // N_KV_HEADS
WINDOW = 128
ATT_BLOCK = 128
ROPE_BASE = 10000.0

Q_WIDTH = N_Q_HEADS * HEAD_DIM
KV_WIDTH = N_KV_HEADS * HEAD_DIM
KV_START = 2 * GM_WIDTH + Q_WIDTH
EVEN_IN = KV_START + 2 * KV_WIDTH
EVEN_MIX = GM_WIDTH + Q_WIDTH

RNN_WIDTH = D_MODEL
RG_HEADS = 4
RG_BLOCK = RNN_WIDTH // RG_HEADS
CONV_W = 4
CONV_LEFT = 1
RG_C = 8.0
ODD_IN = 2 * RNN_WIDTH

kernel_name = "hybrid_gmlp_swa_rglru_prefix_dit"


def rmsnorm(x, g):
    xf = x.astype(jnp.float32)
    y = xf * lax.rsqrt(jnp.mean(xf * xf, axis=-1, keepdims=True) + EPS)
    return (y * g.astype(jnp.float32)).astype(x.dtype)


def modulate(h, shift, scale):
    return h * (1.0 + scale) + shift


def swiglu(h, wg, wu, wd):
    return (jax.nn.silu(h @ wg) * (h @ wu)) @ wd


def ffn_half_step(x, g, shift, scale, gate, wg, wu, wd):
    h = modulate(rmsnorm(x, g), shift, scale)
    return x + 0.5 * gate * swiglu(h, wg, wu, wd)


def rope_tables(n_tokens):
    pos = jnp.arange(n_tokens, dtype=jnp.int32)
    row = (pos // GRID_W).astype(jnp.float32)
    col = (pos % GRID_W).astype(jnp.float32)
    n_freq = HEAD_DIM // 4
    inv = ROPE_BASE ** (-jnp.arange(n_freq, dtype=jnp.float32) / n_freq)
    ang_r = row[:, None] * inv
    ang_c = col[:, None] * inv
    return jnp.cos(ang_r), jnp.sin(ang_r), jnp.cos(ang_c), jnp.sin(ang_c)


def _rot(x, cos, sin):
    x1, x2 = jnp.split(x, 2, axis=-1)
    cos = cos[:, None, :]
    sin = sin[:, None, :]
    return jnp.concatenate([x1 * cos - x2 * sin, x1 * sin + x2 * cos], axis=-1)


def rope_2d(x, tables):
    cr, sr, cc, sc = tables
    xr, xc = jnp.split(x, 2, axis=-1)
    return jnp.concatenate([_rot(xr, cr, sr), _rot(xc, cc, sc)], axis=-1).astype(x.dtype)


def spatial_gating(u, z, gain, ws, bs):
    B, N, _ = u.shape
    z = rmsnorm(z, gain).reshape(B, N // GM_CHUNK, GM_CHUNK, GM_GROUPS, GM_GROUP_CH)
    s = jnp.einsum('gpq,bnqgc->bnpgc', ws, z) + bs.T[None, None, :, :, None]
    return u * s.reshape(B, N, GM_WIDTH)


def window_attention(q, k, v, k_ctx, v_ctx, sink):
    B, S = q.shape[0], q.shape[1]
    L = k_ctx.shape[1]
    nblk = S // ATT_BLOCK
    n_loc = 3 * ATT_BLOCK
    scale = HEAD_DIM ** -0.5
    qb = q.reshape(B, nblk, ATT_BLOCK, N_KV_HEADS, Q_PER_KV, HEAD_DIM)
    pad = ((0, 0), (ATT_BLOCK, ATT_BLOCK), (0, 0), (0, 0))
    kp = jnp.pad(k, pad).reshape(B, nblk + 2, ATT_BLOCK, N_KV_HEADS, HEAD_DIM)
    vp = jnp.pad(v, pad).reshape(B, nblk + 2, ATT_BLOCK, N_KV_HEADS, HEAD_DIM)

    def band(t):
        return jnp.concatenate([t[:, :-2], t[:, 1:-1], t[:, 2:]], axis=2)

    kb, vb = band(kp), band(vp)
    blk = jnp.arange(nblk, dtype=jnp.int32)[:, None]
    qpos = blk * ATT_BLOCK + jnp.arange(ATT_BLOCK, dtype=jnp.int32)[None]
    kpos = (blk - 1) * ATT_BLOCK + jnp.arange(n_loc, dtype=jnp.int32)[None]
    kp_ = kpos[:, None, :]
    mask = (jnp.abs(kp_ - qpos[:, :, None]) <= WINDOW) & (kp_ >= 0) & (kp_ < S)
    sink_g = sink.reshape(N_KV_HEADS, Q_PER_KV).astype(jnp.float32)

    def block(args):
        qj, kj, vj, mj = args
        s_loc = jnp.einsum('bqkgd,bnkd->bkgqn', qj, kj).astype(jnp.float32) * scale
        s_loc = jnp.where(mj, s_loc, NEG_INF)
        s_ctx = jnp.einsum('bqkgd,bnkd->bkgqn', qj, k_ctx).astype(jnp.float32) * scale
        s_sink = jnp.broadcast_to(sink_g[None, :, :, None, None], s_loc.shape[:-1] + (1,))
        p = jax.nn.softmax(jnp.concatenate([s_loc, s_ctx, s_sink], axis=-1), axis=-1)
        p_loc = p[..., :n_loc].astype(vj.dtype)
        p_ctx = p[..., n_loc:n_loc + L].astype(vj.dtype)
        return (jnp.einsum('bkgqn,bnkd->bqkgd', p_loc, vj)
                + jnp.einsum('bkgqn,bnkd->bqkgd', p_ctx, v_ctx))

    o = lax.map(block, (jnp.moveaxis(qb, 1, 0), jnp.moveaxis(kb, 1, 0),
                        jnp.moveaxis(vb, 1, 0), mask))
    return jnp.moveaxis(o, 0, 1).reshape(B, S, Q_WIDTH)


def context_attention(q, k, v, sink):
    B, L = q.shape[0], q.shape[1]
    qg = q.reshape(B, L, N_KV_HEADS, Q_PER_KV, HEAD_DIM)
    s = jnp.einsum('bqkgd,bnkd->bkgqn', qg, k).astype(jnp.float32) * (HEAD_DIM ** -0.5)
    sink_g = sink.reshape(N_KV_HEADS, Q_PER_KV).astype(jnp.float32)
    s_sink = jnp.broadcast_to(sink_g[None, :, :, None, None], s.shape[:-1] + (1,))
    p = jax.nn.softmax(jnp.concatenate([s, s_sink], axis=-1), axis=-1)[..., :L].astype(v.dtype)
    return jnp.einsum('bkgqn,bnkd->bqkgd', p, v).reshape(B, L, Q_WIDTH)


def even_mixer(hx, hc, rope, w_in, w_out, gm_g, gm_ws, gm_bs, sink, need_ctx):
    B, S, _ = hx.shape
    L = hc.shape[1]
    ux, zx, qx, kx, vx = jnp.split(
        hx @ w_in, [GM_WIDTH, 2 * GM_WIDTH, KV_START, KV_START + KV_WIDTH], axis=-1)
    kc, vc = jnp.split(hc @ w_in[:, KV_START:], 2, axis=-1)
    kc = kc.reshape(B, L, N_KV_HEADS, HEAD_DIM)
    vc = vc.reshape(B, L, N_KV_HEADS, HEAD_DIM)
    qx = rope_2d(qx.reshape(B, S, N_Q_HEADS, HEAD_DIM), rope)
    kx = rope_2d(kx.reshape(B, S, N_KV_HEADS, HEAD_DIM), rope)
    vx = vx.reshape(B, S, N_KV_HEADS, HEAD_DIM)
    a_x = spatial_gating(jax.nn.gelu(ux), jax.nn.gelu(zx), gm_g, gm_ws, gm_bs)
    b_x = window_attention(qx, kx, vx, kc, vc, sink)
    yx = jnp.concatenate([a_x, b_x], axis=-1) @ w_out
    if not need_ctx:
        return yx, None
    uc, zc, qc = jnp.split(hc @ w_in[:, :KV_START], [GM_WIDTH, 2 * GM_WIDTH], axis=-1)
    a_c = spatial_gating(jax.nn.gelu(uc), jax.nn.gelu(zc), gm_g, gm_ws, gm_bs)
    b_c = context_attention(qc.reshape(B, L, N_Q_HEADS, HEAD_DIM), kc, vc, sink)
    yc = jnp.concatenate([a_c, b_c], axis=-1) @ w_out
    return yx, yc


def centred_dwconv(x, w, b):
    C = x.shape[-1]
    y = lax.conv_general_dilated(
        x, w[:, None, :], window_strides=(1,),
        padding=[(CONV_LEFT, CONV_W - 1 - CONV_LEFT)],
        dimension_numbers=('NWC', 'WIO', 'NWC'), feature_group_count=C)
    return y + b


def block_diag(x, w, b):
    B, N, _ = x.shape
    xh = x.reshape(B, N, RG_HEADS, RG_BLOCK)
    return jnp.einsum('bnhi,hij->bnhj', xh, w).reshape(B, N, RNN_WIDTH) + b


def rglru_coeffs(x, wa, ba, wx, bx, lam):
    r = jax.nn.sigmoid(block_diag(x, wa, ba)).astype(jnp.float32)
    i = jax.nn.sigmoid(block_diag(x, wx, bx)).astype(jnp.float32)
    log_a = -RG_C * r * jax.nn.softplus(-lam.astype(jnp.float32))
    a = jnp.exp(log_a)
    mult = jnp.sqrt(-jnp.expm1(2.0 * log_a))
    return a, mult * (i * x.astype(jnp.float32))


def _combine(e1, e2):
    a1, b1 = e1
    a2, b2 = e2
    return a1 * a2, a2 * b1 + b2


def linear_scan(a, b, h0):
    b = b.at[:, 0].add(a[:, 0] * h0)
    _, h = lax.associative_scan(_combine, (a, b), axis=1)
    return h


def rglru_bidir(xc, wa, ba, wx, bx, lam, h0_f, h0_b):
    a_f, b_f = rglru_coeffs(xc, wa[0], ba[0], wx[0], bx[0], lam[0])
    h_f = linear_scan(a_f, b_f, h0_f)
    a_b, b_b = rglru_coeffs(xc, wa[1], ba[1], wx[1], bx[1], lam[1])
    h_b = jnp.flip(linear_scan(jnp.flip(a_b, 1), jnp.flip(b_b, 1), h0_b), 1)
    return h_f, h_b


def odd_mixer(hx, hc, w_in, w_out, cw, cb, wa, ba, wx, bx, lam, need_ctx):
    B = hx.shape[0]
    gx, xx = jnp.split(hx @ w_in, 2, axis=-1)
    xc = hc @ w_in[:, RNN_WIDTH:]
    zero = jnp.zeros((B, RNN_WIDTH), jnp.float32)
    hf_c, hb_c = rglru_bidir(centred_dwconv(xc, cw, cb), wa, ba, wx, bx, lam, zero, zero)
    hf_x, hb_x = rglru_bidir(centred_dwconv(xx, cw, cb), wa, ba, wx, bx, lam,
                             hf_c[:, -1], hb_c[:, 0])
    yx = ((hf_x + hb_x).astype(hx.dtype) * jax.nn.gelu(gx)) @ w_out
    if not need_ctx:
        return yx, None
    gc = hc @ w_in[:, :RNN_WIDTH]
    yc = ((hf_c + hb_c).astype(hc.dtype) * jax.nn.gelu(gc)) @ w_out
    return yx, yc


def setup_inputs(seed: int = 0) -> dict:
    key = jax.random.key(seed)
    ks = jax.random.split(key, 32)
    f32 = jnp.float32
    n = lambda k, s, sc: jax.random.normal(k, s, f32) * sc
    a_init = jax.random.uniform(ks[25], (N_ODD, 2, RNN_WIDTH), f32, 0.9, 0.999)
    return {
        "x": n(ks[0], (BATCH, SEQ, D_MODEL), 1.0),
        "c": n(ks[1], (BATCH, D_MODEL), 1.0),
        "ctx": n(ks[2], (BATCH, CTX_LEN, D_MODEL), 1.0),
        "c_ctx": n(ks[3], (D_MODEL,), 1.0),
        "ada_w": n(ks[4], (DEPTH, D_MODEL, N_MOD * D_MODEL), 0.5 * D_MODEL ** -0.5),
        "ada_b": n(ks[5], (DEPTH, N_MOD * D_MODEL), 0.02),
        "norm_g": 1.0 + n(ks[6], (DEPTH, 3, D_MODEL), 0.02),
        "ffn_w_gate": n(ks[7], (DEPTH, 2, D_MODEL, D_FF), D_MODEL ** -0.5),
        "ffn_w_up": n(ks[8], (DEPTH, 2, D_MODEL, D_FF), D_MODEL ** -0.5),
        "ffn_w_down": n(ks[9], (DEPTH, 2, D_FF, D_MODEL), D_FF ** -0.5),
        "ev_w_in": n(ks[10], (N_EVEN, D_MODEL, EVEN_IN), D_MODEL ** -0.5),
        "ev_w_out": n(ks[11], (N_EVEN, EVEN_MIX, D_MODEL), EVEN_MIX ** -0.5),
        "gm_norm_g": 1.0 + n(ks[12], (N_EVEN, GM_WIDTH), 0.02),
        "gm_ws": n(ks[13], (N_EVEN, GM_GROUPS, GM_CHUNK, GM_CHUNK), GM_CHUNK ** -0.5),
        "gm_bs": 1.0 + n(ks[14], (N_EVEN, GM_GROUPS, GM_CHUNK), 0.02),
        "attn_sink": n(ks[15], (N_EVEN, N_Q_HEADS), 1.0),
        "od_w_in": n(ks[16], (N_ODD, D_MODEL, ODD_IN), D_MODEL ** -0.5),
        "od_w_out": n(ks[17], (N_ODD, RNN_WIDTH, D_MODEL), RNN_WIDTH ** -0.5),
        "conv_w": n(ks[18], (N_ODD, CONV_W, RNN_WIDTH), CONV_W ** -0.5),
        "conv_b": n(ks[19], (N_ODD, RNN_WIDTH), 0.02),
        "rg_wa": n(ks[20], (N_ODD, 2, RG_HEADS, RG_BLOCK, RG_BLOCK), RG_BLOCK ** -0.5),
        "rg_ba": n(ks[21], (N_ODD, 2, RNN_WIDTH), 0.02),
        "rg_wx": n(ks[22], (N_ODD, 2, RG_HEADS, RG_BLOCK, RG_BLOCK), RG_BLOCK ** -0.5),
        "rg_bx": n(ks[23], (N_ODD, 2, RNN_WIDTH), 0.02),
        "rg_lambda": jnp.log(a_init) - jnp.log1p(-a_init),
        "final_norm_g": 1.0 + n(ks[24], (D_MODEL,), 0.02),
    }


def reference(x, c, ctx, c_ctx, ada_w, ada_b, norm_g, ffn_w_gate, ffn_w_up, ffn_w_down,
              ev_w_in, ev_w_out, gm_norm_g, gm_ws, gm_bs, attn_sink,
              od_w_in, od_w_out, conv_w, conv_b, rg_wa, rg_ba, rg_wx, rg_bx, rg_lambda,
              final_norm_g):
    S = x.shape[1]
    rope = rope_tables(S)
    for l in range(DEPTH):
        last = l == DEPTH - 1
        mx = [t[:, None, :] for t in jnp.split(jax.nn.silu(c) @ ada_w[l] + ada_b[l], N_MOD, axis=-1)]
        mc = jnp.split(jax.nn.silu(c_ctx) @ ada_w[l] + ada_b[l], N_MOD, axis=-1)
        x = ffn_half_step(x, norm_g[l, 0], mx[0], mx[1], mx[2],
                          ffn_w_gate[l, 0], ffn_w_up[l, 0], ffn_w_down[l, 0])
        ctx = ffn_half_step(ctx, norm_g[l, 0], mc[0], mc[1], mc[2],
                            ffn_w_gate[l, 0], ffn_w_up[l, 0], ffn_w_down[l, 0])
        hx = modulate(rmsnorm(x, norm_g[l, 1]), mx[3], mx[4])
        hc = modulate(rmsnorm(ctx, norm_g[l, 1]), mc[3], mc[4])
        if l % 2 == 0:
            e = l // 2
            yx, yc = even_mixer(hx, hc, rope, ev_w_in[e], ev_w_out[e], gm_norm_g[e],
                                gm_ws[e], gm_bs[e], attn_sink[e], not last)
        else:
            o = l // 2
            yx, yc = odd_mixer(hx, hc, od_w_in[o], od_w_out[o], conv_w[o], conv_b[o],
                               rg_wa[o], rg_ba[o], rg_wx[o], rg_bx[o], rg_lambda[o], not last)
        x = x + mx[5] * yx
        x = ffn_half_step(x, norm_g[l, 2], mx[6], mx[7], mx[8],
                          ffn_w_gate[l, 1], ffn_w_up[l, 1], ffn_w_down[l, 1])
        if not last:
            ctx = ctx + mc[5] * yc
            ctx = ffn_half_step(ctx, norm_g[l, 2], mc[6], mc[7], mc[8],
                                ffn_w_gate[l, 1], ffn_w_up[l, 1], ffn_w_down[l, 1])
    return rmsnorm(x, final_norm_g)
```

```python
import numpy as np
import ml_dtypes
from contextlib import ExitStack
import concourse.bass as bass
import concourse.mybir as mybir
from concourse.bass_utils import run_bass_kernel_spmd

F32 = mybir.dt.float32
BF16 = mybir.dt.bfloat16
AF = mybir.ActivationFunctionType
ALU = mybir.AluOpType
AX = mybir.AxisListType

D = 1024
DFF = 2816
NJ = DFF // 128
SEQ = 16384
TOK = 4096
HALO = 128
NTH = TOK + 2 * HALO
LCTX = 256
EPS = 1e-6


class Prog:
    ENG = ('sp', 'act', 'dve', 'pool', 'pe')

    def __init__(self, nc, same_engine_sync=True):
        self.nc = nc
        self.ops = {e: [] for e in self.ENG}
        self.esem = {}
        self.ecnt = {}
        self.seen = {}
        self.last_w = {}
        self.readers = {}
        self.dsem = {}
        self.dcnt = {}
        self.same = same_engine_sync
        self.nsem = 0
        self.epoch()

    def _newsem(self, name):
        self.nsem += 1
        return self.nc.alloc_semaphore(f"{name}_{self.nsem}")

    def epoch(self):
        for e in self.ENG:
            self.esem[e] = self._newsem("e" + e)
            self.ecnt[e] = 0

    def op(self, eng, fn, reads=(), writes=(), dma=None, signal=True, extra=(), inc=16, nodep=False):
        deps = list(extra)
        if not nodep:
            for k in reads:
                if k in self.last_w:
                    deps.append(self.last_w[k])
            for k in writes:
                if k in self.last_w:
                    deps.append(self.last_w[k])
                deps.extend(self.readers.get(k, ()))
        m = {}
        for (s, v, pe_) in deps:
            if pe_ == eng and (eng == 'pe' or not self.same):
                continue
            key = (eng, id(s))
            if self.seen.get(key, (None, 0))[1] >= v:
                continue
            self.seen[key] = (s, v)
            m[id(s)] = (s, v)
        waits = list(m.values())
        if dma is not None:
            if dma not in self.dsem:
                self.dsem[dma] = self._newsem("d")
                self.dcnt[dma] = 0
            self.dcnt[dma] += inc
            sig = (self.dsem[dma], inc)
            tok = (self.dsem[dma], self.dcnt[dma], 'dma')
        elif signal:
            self.ecnt[eng] += 1
            sig = (self.esem[eng], 1)
            tok = (self.esem[eng], self.ecnt[eng], eng)
        else:
            sig = None
            tok = (self.esem[eng], self.ecnt[eng] + 1, eng)
        self.ops[eng].append((waits, fn, sig))
        for k in writes:
            self.last_w[k] = tok
            self.readers[k] = []
        for k in reads:
            self.readers.setdefault(k, []).append(tok)
        return tok

    def barrier(self):
        toks = [(self.esem[e], self.ecnt[e], e) for e in self.ENG if self.ecnt[e] > 0]
        toks += [(self.dsem[k], self.dcnt[k], 'dma') for k in self.dsem
                 if not (isinstance(k, tuple) and k[0] == 'wcast')]
        for e in self.ENG:
            self.op(e, None, extra=[t for t in toks if t[2] != e or e != 'pe'], signal=False)

    def emit(self):
        nc = self.nc
        with nc.Block() as block:
            for e, deco in (('sp', block.sync), ('act', block.scalar), ('dve', block.vector),
                            ('pool', block.gpsimd), ('pe', block.tensor)):
                lst = self.ops[e]

                def body(engine, lst=lst):
                    for waits, fn, sig in lst:
                        for (s, v) in waits:
                            engine.wait_ge(s, v)
                        if fn is None:
                            continue
                        ins = fn(engine)
                        if sig is not None:
                            ins.then_inc(sig[0], sig[1])
                deco(body)


class Ctx:
    pass


NAMES = []


def build(stages=("ffn00", "mix0", "ffn01", "ffn10", "mix1", "ffn11"), nomod=False):
    nc = bass.Bass("TRN2", target_bir_lowering=False)
    P = Prog(nc)
    K = Ctx()
    K.nc, K.P = nc, P

    NAMES.clear()

    def din(name, shape, dt=F32):
        NAMES.append(name)
        return nc.dram_tensor(name, list(shape), dt, kind="ExternalInput").ap()

    def dscr(name, shape, dt=F32):
        return nc.dram_tensor(name, list(shape), dt, kind="Internal").ap()

    K.xh = din("xh", [NTH, D])
    K.ctx = din("ctxb", [LCTX, D])
    K.ccT = din("ccT", [128, 8, 2])
    K.ident = din("ident", [128, 128])
    K.nomod = nomod
    if not nomod:
        K.ada_w = [din(f"ada_w{l}", [D, 9 * D]) for l in range(2)]
        K.ada_b = din("ada_b", [2, 9 * D])
        K.norm_g = din("norm_g", [2, 3, D])
    K.ffn_groups = [(l, i) for l in range(2) for i in range(2) if f"ffn{l}{i}" in stages]
    K.wg = {g: din(f"wg{g[0]}{g[1]}", [D, DFF]) for g in K.ffn_groups}
    K.wu = {g: din(f"wu{g[0]}{g[1]}", [D, DFF]) for g in K.ffn_groups}
    K.wd = {g: din(f"wd{g[0]}{g[1]}", [DFF, D]) for g in K.ffn_groups}
    K.fng = din("final_norm_g", [D])
    if "mix0" in stages:
        K.ev_w_in = din("ev_w_in", [D, 1792])
        K.ev_w_out = din("ev_w_out", [D, D])
        K.gm_norm_g = din("gm_norm_g", [1, 512])
        K.gm_ws = din("gm_ws", [4, 128, 128])
        K.gm_bs = din("gm_bs", [1, 4, 128])
        K.attn_sink = din("attn_sink", [1, 8])
        K.ropeC = din("ropeC", [64, NBLK * 128])
        K.ropeS = din("ropeS", [64, NBLK * 128])
        K.amask = din("amask", [4, 128, 512])
        K.winb = dscr("winb", [128, 8, 1792], BF16)
        K.woutb = dscr("woutb", [128, 8, D], BF16)
    if "mix1" in stages:
        K.od_w_in = din("od_w_in", [D, 2048])
        K.od_w_out = din("od_w_out", [D, D])
        K.rg_wa = din("rg_wa", [2, 4, 256, 256])
        K.rg_wx = din("rg_wx", [2, 4, 256, 256])
        K.odcols = din("odcols", [128, 11, 8])
        K.selc = din("selc", [128, 24])
        K.owinb = dscr("owinb", [128, 8, 2048], BF16)
        K.owoutb = dscr("owoutb", [128, 8, D], BF16)
        K.wab = dscr("wab", [128, 16, 256], BF16)
        K.wxb = dscr("wxb", [128, 16, 256], BF16)
        K.gg = dscr("gg", [128, 8, TOK], BF16)
        K.xxs = dscr("xxs", [128, 8, TOK + 3])
        K.xxc = dscr("xxc", [128, 8, LCTX + 3])
        K.xcs = dscr("xcs", [128, 8, TOK])
        K.hfs = dscr("hfs", [128, 8, TOK])
        K.ad = [dscr(f"ad{d}", [128, 8, TOK]) for d in range(2)]
        K.bd = [dscr(f"bd{d}", [128, 8, TOK]) for d in range(2)]
        K.bnc1 = dscr("bnc1", [3, D])
        K.gath1 = dscr("gath1", [12, D])
        K.bnc2 = dscr("bnc2", [4, D])
        K.gath2 = dscr("gath2", [16, D])
    K.out = nc.dram_tensor("out", [TOK, D], F32, kind="ExternalOutput").ap()

    K.wgb = [[dscr(f"wgb{l}{i}", [NJ, 128, 8, 128], BF16) for i in range(2)] for l in range(2)]
    K.wub = [[dscr(f"wub{l}{i}", [NJ, 128, 8, 128], BF16) for i in range(2)] for l in range(2)]
    K.wdb = [[dscr(f"wdb{l}{i}", [DFF, D], BF16) for i in range(2)] for l in range(2)]
    K.modrows = dscr("modrows", [2, 2, 9 * D])
    K.xa = dscr("xa", [NTH, D])
    K.xb = dscr("xb", [NTH, D])
    K.ca = dscr("ca", [LCTX, D])
    K.cb = dscr("cb", [LCTX, D])

    es = ExitStack()
    K.es = es

    uid = [0]

    def sb(name, shape, dt=F32, stack=None):
        uid[0] += 1
        return (stack or es).enter_context(nc.sbuf_tensor(f"{name}_{uid[0]}", list(shape), dt))

    def ps(name, shape, dt=F32, stack=None):
        return (stack or es).enter_context(nc.psum_tensor(name, list(shape), dt))
    K.sb, K.ps = sb, ps

    K.identf = sb("identf", [128, 128])
    P.op('sp', lambda e: e.dma_start(out=K.identf[:], in_=K.ident[:, :]), writes=['identf'], dma='identf')
    K.epsc = sb("epsc", [128, 1])
    P.op('dve', lambda e: e.memset(K.epsc[:], EPS), writes=['epsc'])
    K.Acol = sb("Acol", [128, 2, 2, 3, 8])
    K.Bcol = sb("Bcol", [128, 2, 2, 3, 8])

    K.bank = [ps(f"bank{i}", [128, 512]) for i in range(8)]

    cast_weights(K)
    if nomod:
        P.op('dve', lambda e: e.memset(K.Acol[:], 1.0), writes=['Acol'])
        P.op('dve', lambda e: e.memset(K.Bcol[:], 0.0), writes=['Bcol'])
    else:
        prologue_mods(K)

    src_x, src_c = K.xh, K.ctx
    pp = [0]

    def nxt():
        pp[0] ^= 1
        return (K.xa, K.ca) if pp[0] else (K.xb, K.cb)
    if "ffn00" in stages:
        dx, dc = nxt()
        ffn_phase(K, 0, 0, src_x, dx, 0, NTH, src_c, dc)
        src_x, src_c = dx, dc
    if "mix0" in stages:
        dx, dc = nxt()
        even_mixer_phase(K, src_x, dx, src_c, dc)
        src_x, src_c = dx, dc
    if "ffn01" in stages:
        dx, dc = nxt()
        ffn_phase(K, 0, 1, src_x, dx, HALO, TOK, src_c, dc)
        src_x, src_c = dx, dc
    if "ffn10" in stages:
        dx, dc = nxt()
        ffn_phase(K, 1, 0, src_x, dx, HALO, TOK, src_c, dc)
        src_x, src_c = dx, dc
    if "mix1" in stages:
        dx, dc = nxt()
        odd_mixer_phase(K, src_x, dx, src_c)
        src_x, src_c = dx, dc
    if "ffn11" in stages:
        dx, dc = nxt()
        ffn_phase(K, 1, 1, src_x, dx, HALO, TOK, None, None, fuse_final=True)
        src_x, src_c = dx, dc
    else:
        final_phase(K, src_x)
    global LASTP, LASTNAMES
    LASTP = P
    LASTNAMES = set(NAMES)
    P.emit()
    return nc


def cast_weights(K):
    nc, P = K.nc, K.P
    def ffn_cast(l, i):
            if True:
                key = ('wcast', l, i)
                for (src, dst) in ((K.wg, K.wgb), (K.wu, K.wub)):
                    for j in range(NJ):
                        s_ap = src[(l, i)][:, j * 128:(j + 1) * 128].rearrange("(k p) m -> p k m", p=128)
                        d_ap = dst[l][i][j]
                        P.op('pool', lambda e, s_ap=s_ap, d_ap=d_ap: e.dma_start(out=d_ap, in_=s_ap),
                             writes=[key], dma=key, nodep=True)
                for r in range(8):
                    rs = DFF // 8
                    s_ap = K.wd[(l, i)][r * rs:(r + 1) * rs, :]
                    d_ap = K.wdb[l][i][r * rs:(r + 1) * rs, :]
                    P.op('pool', lambda e, s_ap=s_ap, d_ap=d_ap: e.dma_start(out=d_ap, in_=s_ap),
                         writes=[key], dma=key, nodep=True)


    done = set()
    for g in [(0, 0)]:
        if g in K.ffn_groups:
            ffn_cast(*g); done.add(g)
    if hasattr(K, "ev_w_in"):
        key = ('wcast', 'ev')
        for k in range(8):
            P.op('pool', lambda e, k=k: e.dma_start(out=K.winb[:, k, :], in_=K.ev_w_in[k * 128:(k + 1) * 128, :]),
                 writes=[key], dma=key, nodep=True)
            P.op('pool', lambda e, k=k: e.dma_start(out=K.woutb[:, k, :], in_=K.ev_w_out[k * 128:(k + 1) * 128, :]),
                 writes=[key], dma=key, nodep=True)
    if hasattr(K, "od_w_in"):
        key = ('wcast', 'od')
        for k in range(8):
            P.op('pool', lambda e, k=k: e.dma_start(out=K.owinb[:, k, :], in_=K.od_w_in[k * 128:(k + 1) * 128, :]),
                 writes=[key], dma=key, nodep=True)
            P.op('pool', lambda e, k=k: e.dma_start(out=K.owoutb[:, k, :], in_=K.od_w_out[k * 128:(k + 1) * 128, :]),
                 writes=[key], dma=key, nodep=True)
        for (srcw, dstw) in ((K.rg_wa, K.wab), (K.rg_wx, K.wxb)):
            P.op('pool', lambda e, srcw=srcw, dstw=dstw: e.dma_start(
                out=dstw[:, :, :], in_=srcw.rearrange("d h (kc p) o -> p (d h kc) o", p=128)),
                writes=[key], dma=key, nodep=True)

    for g in K.ffn_groups:
        if g not in done:
            ffn_cast(*g)


def prologue_mods(K):
    nc, P = K.nc, K.P
    st = ExitStack()
    sT = K.sb("sT", [128, 8, 2], stack=st)
    ccs = K.sb("ccs", [128, 8, 2], stack=st)
    ones = K.sb("ones", [128, 64], stack=st)
    sTrep = K.sb("sTrep", [128, 8, 128], BF16, stack=st)
    P.op('dve', lambda e: e.memset(ones[:], 1.0), writes=['ones'])
    P.op('sp', lambda e: e.dma_start(out=ccs[:], in_=K.ccT[:, :, :]), writes=['ccs'], dma='ccs')
    P.op('act', lambda e: e.activation(out=sT[:], in_=ccs[:], func=AF.Silu), reads=['ccs'], writes=['sT'])
    for k in range(8):
        for t in range(2):
            P.op('act', lambda e, k=k, t=t: e.activation(out=sTrep[:, k, t * 64:(t + 1) * 64], in_=ones[:], func=AF.Copy,
                                                         scale=sT[:, k, t:t + 1]),
                 reads=['sT', 'ones'], writes=['sTrep'])
    rows = K.sb("rows", [128, 9 * D], stack=st)
    brow = K.sb("brow", [128, 9 * D], stack=st)
    aw = [K.sb(f"aw{s}", [128, 3072], stack=st) for s in range(2)]
    awb = [K.sb(f"awb{s}", [128, 3072], BF16, stack=st) for s in range(2)]
    cols = K.sb("cols", [128, 2, 2, 9, 8], stack=st)
    ng = K.sb("ng", [128, 2, 3, 8], stack=st)
    for l in range(2):
        for i in range(3):
            P.op('sp', lambda e, l=l, i=i: e.dma_start(
                out=ng[:, l, i, :], in_=K.norm_g[l, i, :].rearrange("(k p) -> p k", p=128),
                allow_slow_non_contiguous=True), writes=['ng'], dma='ng', nodep=True)
    n = 0
    for l in range(2):
        P.op('sp', lambda e, l=l: e.dma_start(out=brow[:], in_=K.ada_b[l:l + 1, :].partition_broadcast(128)),
             writes=['brow'], dma='brow')
        for cg in range(3):
            for k in range(8):
                s = n % 2
                n += 1
                P.op('sp', lambda e, l=l, cg=cg, k=k, s=s: e.dma_start(
                    out=aw[s][:], in_=K.ada_w[l][k * 128:(k + 1) * 128, cg * 3072:(cg + 1) * 3072]),
                    writes=[('aw', s)], dma=('aw', s))
                if n % 2:
                    P.op('dve', lambda e, s=s: e.tensor_copy(out=awb[s][:], in_=aw[s][:]), reads=[('aw', s)], writes=[('awb', s)])
                else:
                    P.op('act', lambda e, s=s: e.activation(out=awb[s][:], in_=aw[s][:], func=AF.Copy), reads=[('aw', s)], writes=[('awb', s)])
                for b in range(6):
                    P.op('pe', lambda e, k=k, s=s, b=b: e.matmul(
                        K.bank[b][:, :], lhsT=sTrep[:, k, :], rhs=awb[s][:, b * 512:(b + 1) * 512],
                        start=(k == 0), stop=(k == 7)),
                        reads=['sTrep', ('awb', s)], writes=[('bank', b)], signal=(b == 5))
            for b in range(6):
                c0 = cg * 3072 + b * 512
                P.op('dve', lambda e, b=b, c0=c0: e.tensor_tensor(
                    out=rows[:, c0:c0 + 512], in0=K.bank[b][:, :], in1=brow[:, c0:c0 + 512], op=ALU.add),
                    reads=[('bank', b), 'brow'], writes=['rows'])
        P.op('sp', lambda e, l=l: e.dma_start(out=K.modrows[l, 0:1, :], in_=rows[0:1, :]),
             reads=['rows'], writes=[('modrows', l)], dma='modrows')
        P.op('sp', lambda e, l=l: e.dma_start(out=K.modrows[l, 1:2, :], in_=rows[64:65, :]),
             reads=['rows'], writes=[('modrows', l)], dma='modrows')
        tb = 0
        for m in (0, 1, 3, 4, 6, 7):
            for kh in range(2):
                bki = 6 + tb % 2
                tb += 1
                for kk in range(4):
                    c0 = m * D + (kh * 4 + kk) * 128
                    P.op('pe', lambda e, bki=bki, kk=kk, c0=c0: e.transpose(
                        out=K.bank[bki][:, kk * 128:(kk + 1) * 128], in_=rows[:, c0:c0 + 128], identity=K.identf[:]),
                        reads=['rows', 'identf'], writes=[('bank', bki)], signal=(kk == 3))
                bv = K.bank[bki][:, :].rearrange("p (a b) -> p a b", a=4)
                for t in range(2):
                    P.op('dve', lambda e, bv=bv, t=t, m=m, kh=kh, l=l: e.tensor_copy(
                        out=cols[:, l, t, m, kh * 4:(kh + 1) * 4], in_=bv[:, :, 64 * t]),
                        reads=[('bank', bki)], writes=['cols'])
    for l in range(2):
        for t in range(2):
            for i in range(3):
                P.op('dve', lambda e, l=l, t=t, i=i: e.scalar_tensor_tensor(
                    out=K.Acol[:, l, t, i, :], in0=cols[:, l, t, 3 * i + 1, :], scalar=1.0,
                    in1=ng[:, l, i, :], op0=ALU.add, op1=ALU.mult),
                    reads=['cols', 'ng'], writes=['Acol'])
                P.op('dve', lambda e, l=l, t=t, i=i: e.tensor_copy(
                    out=K.Bcol[:, l, t, i, :], in_=cols[:, l, t, 3 * i, :]),
                    reads=['cols'], writes=['Bcol'])
    P.barrier()
    st.close()


def norm_load(K, src, r0, nb, xs):
    P = K.P
    for bi in range(nb):
        P.op('sp', lambda e, bi=bi: e.dma_start(out=xs[:, bi, :], in_=src[r0 + bi * 128:r0 + (bi + 1) * 128, :]),
             writes=[('xs', bi)], dma=('xs', bi))


def norm_stats(K, nb, xs):
    P = K.P
    ss_t, rstd_t, junk_t = K.ss, K.rstd, K.junk
    for bi in range(nb):
        P.op('act', lambda e, bi=bi: e.activation(out=junk_t[:], in_=xs[:, bi, :], func=AF.Square,
                                                  accum_out=ss_t[:, bi:bi + 1]),
             reads=[('xs', bi)], writes=['junk', ('ss', bi)])
        P.op('act', lambda e, bi=bi: e.activation(out=ss_t[:, bi:bi + 1], in_=ss_t[:, bi:bi + 1], func=AF.Sqrt,
                                                  bias=K.epsc[:], scale=1.0 / D),
             reads=[('ss', bi), 'epsc'], writes=[('ss', bi)])
        P.op('dve', lambda e, bi=bi: e.reciprocal(out=rstd_t[:, bi:bi + 1], in_=ss_t[:, bi:bi + 1]),
             reads=[('ss', bi)], writes=[('rstd', bi)])
        P.op('act', lambda e, bi=bi: e.activation(out=xs[:, bi, :], in_=xs[:, bi, :], func=AF.Copy,
                                                  scale=rstd_t[:, bi:bi + 1]),
             reads=[('xs', bi), ('rstd', bi)], writes=[('xs', bi)])


def norm_T(K, nb, xs, l, t, i, xnT):
    P = K.P
    T = nb * 128
    nh = (T + 511) // 512
    for k in range(8):
        for h in range(nh):
            w = min(512, T - h * 512)
            bk = K.bank[(k * nh + h) % 2]
            bkey = ('bank', (k * nh + h) % 2)
            nbh = w // 128
            for bb in range(nbh):
                bi = h * 4 + bb
                P.op('pe', lambda e, bi=bi, bb=bb, k=k, bk=bk: e.transpose(
                    out=bk[:, bb * 128:(bb + 1) * 128], in_=xs[:, bi, k * 128:(k + 1) * 128], identity=K.identf[:]),
                    reads=[('xs', bi), 'identf'], writes=[bkey], signal=(bb == nbh - 1))
            P.op('act', lambda e, k=k, h=h, w=w, bk=bk: e.activation(
                out=xnT[:, k, h * 512:h * 512 + w], in_=bk[:, 0:w], func=AF.Identity,
                bias=K.Bcol[:, l, t, i, k:k + 1], scale=K.Acol[:, l, t, i, k:k + 1]),
                reads=[bkey, 'Acol', 'Bcol'], writes=['xnT'])


def norm_tile(K, tag, src, r0, nb, xs, l, t, i, xnT):
    norm_load(K, src, r0, nb, xs)
    norm_stats(K, nb, xs)
    norm_T(K, nb, xs, l, t, i, xnT)


def ffn_phase(K, l, i, src, dst, rstart, ntok, srcc, dstc, fuse_final=False):
    nc, P = K.nc, K.P
    st = ExitStack()
    tag = f"f{l}{i}"
    ni = 0 if i == 0 else 2
    TT = 1024
    wd_sb = K.sb("wd_sb", [128, NJ, D], BF16, stack=st)
    hT = K.sb("hT", [128, NJ, TT], BF16, stack=st)
    xnT = K.sb("xnT", [128, 8, TT], BF16, stack=st)
    xs = K.sb("xs", [128, TT // 128, D], stack=st)
    K.junk = K.sb("junk", [128, D], BF16, stack=st)
    K.ss = K.sb("ss", [128, 8], stack=st)
    K.rstd = K.sb("rstd", [128, 8], stack=st)
    wgu = [K.sb(f"wgu{s}", [128, 2, 8, 128], BF16, stack=st) for s in range(3)]
    sg = [K.sb(f"sg{s}", [128, 512], stack=st) for s in range(2)]
    G = K.sb("G", [128, 2, D], stack=st)
    xr = [K.sb(f"xr{s}", [128, D], stack=st) for s in range(2)]
    tmp = [K.sb(f"tmp{s}", [128, D], stack=st) for s in range(2)]

    wkey = ('wcast', l, i)
    P.op('sp', lambda e: e.dma_start(out=wd_sb[:], in_=K.wdb[l][i].rearrange("(j p) f -> p j f", p=128)),
         reads=[wkey], writes=['wd_sb'], dma='wd_sb')
    for t in range(2):
        if K.nomod:
            P.op('dve', lambda e, t=t: e.memset(G[:, t, :], 1.0), writes=['G'])
            continue
        P.op('sp', lambda e, t=t: e.dma_start(
            out=G[:, t, :], in_=K.modrows[l, t:t + 1, (2 + 6 * i) * D:(3 + 6 * i) * D].partition_broadcast(128)),
            reads=[('modrows', l)], writes=['G'], dma='G')

    tiles = []
    r = rstart
    while r < rstart + ntok:
        nbk = min(TT, rstart + ntok - r) // 128
        tiles.append((src, dst, r, nbk, 0))
        r += nbk * 128
    if srcc is not None:
        tiles.append((srcc, dstc, 0, LCTX // 128, 1))
    wn = 0
    on = 0
    pc = 0
    if fuse_final:
        P.op('sp', lambda e: e.dma_start(out=G[:, 1, :], in_=K.fng[None, :].partition_broadcast(128)), reads=['G'], writes=['G'], dma='G')
        fss = K.sb("fss2", [128, 2], stack=st)
        frs = K.sb("frs2", [128, 2], stack=st)
        junk_f = K.junk
    norm_load(K, tiles[0][0], tiles[0][2], tiles[0][3], xs)
    norm_stats(K, tiles[0][3], xs)
    norm_T(K, tiles[0][3], xs, l, tiles[0][4], ni, xnT)
    for ti, (s_ap, d_ap, r0, nb, t) in enumerate(tiles):
        T = nb * 128
        nxt_t = tiles[ti + 1] if ti + 1 < len(tiles) else None
        if nxt_t is not None:
            norm_load(K, nxt_t[0], nxt_t[2], nxt_t[3], xs)
        for j in range(NJ):
            if j == 6 and nxt_t is not None:
                norm_stats(K, nxt_t[3], xs)
            s = wn % 3
            wn += 1
            P.op('sp', lambda e, s=s, j=j: e.dma_start(out=wgu[s][:, 0], in_=K.wgb[l][i][j]),
                 reads=[wkey], writes=[('wgu', s, 0)], dma=('wgu', s, 0))
            P.op('sp', lambda e, s=s, j=j: e.dma_start(out=wgu[s][:, 1], in_=K.wub[l][i][j]),
                 reads=[wkey], writes=[('wgu', s, 1)], dma=('wgu', s, 1))
            for hf in range((T + 511) // 512):
                w = min(512, T - hf * 512)
                p2 = pc % 2
                pc += 1
                bg, bu = K.bank[2 + 2 * p2], K.bank[3 + 2 * p2]
                kg, ku = ('bank', 2 + 2 * p2), ('bank', 3 + 2 * p2)
                for (g_or_u, bk, kk) in ((0, bg, kg), (1, bu, ku)):
                    for k in range(8):
                        P.op('pe', lambda e, s=s, k=k, bk=bk, g_or_u=g_or_u, w=w, hf=hf: e.matmul(
                            bk[:, 0:w], lhsT=wgu[s][:, g_or_u, k, :], rhs=xnT[:, k, hf * 512:hf * 512 + w], start=(k == 0), stop=(k == 7)),
                            reads=[('wgu', s, g_or_u), 'xnT'], writes=[kk], signal=(k == 7))
                P.op('act', lambda e, bg=bg, p2=p2, w=w: e.activation(out=sg[p2][:, 0:w], in_=bg[:, 0:w], func=AF.Silu),
                     reads=[kg], writes=[('sg', p2)])
                P.op('dve', lambda e, bu=bu, p2=p2, j=j, w=w, hf=hf: e.tensor_tensor(
                    out=hT[:, j, hf * 512:hf * 512 + w], in0=bu[:, 0:w], in1=sg[p2][:, 0:w], op=ALU.mult),
                    reads=[ku, ('sg', p2)], writes=['hT'])
        for bi in range(nb):
            so = on % 2
            on += 1
            P.op('sp', lambda e, so=so, bi=bi, s_ap=s_ap, r0=r0: e.dma_start(
                out=xr[so][:], in_=s_ap[r0 + bi * 128:r0 + (bi + 1) * 128, :]),
                writes=[('xr', so)], dma=('xr', so))
            for fh in range(2):
                bk = K.bank[6 + fh]
                kk = ('bank', 6 + fh)
                for j in range(NJ):
                    P.op('pe', lambda e, bk=bk, j=j, bi=bi, fh=fh: e.matmul(
                        bk[:, :], lhsT=hT[:, j, bi * 128:(bi + 1) * 128], rhs=wd_sb[:, j, fh * 512:(fh + 1) * 512],
                        start=(j == 0), stop=(j == NJ - 1)),
                        reads=['hT', 'wd_sb'], writes=[kk], signal=(j == NJ - 1))
                P.op('dve', lambda e, bk=bk, so=so, fh=fh, t=t: e.tensor_tensor(
                    out=tmp[so][:, fh * 512:(fh + 1) * 512], in0=bk[:, :], in1=G[:, t, fh * 512:(fh + 1) * 512],
                    op=ALU.mult), reads=[kk, 'G'], writes=[('tmp', so, fh)])
                P.op('dve', lambda e, so=so, fh=fh: e.scalar_tensor_tensor(
                    out=tmp[so][:, fh * 512:(fh + 1) * 512], in0=tmp[so][:, fh * 512:(fh + 1) * 512], scalar=0.5,
                    in1=xr[so][:, fh * 512:(fh + 1) * 512], op0=ALU.mult, op1=ALU.add),
                    reads=[('tmp', so, fh), ('xr', so)], writes=[('tmp', so, fh)])
            if fuse_final:
                tk = [('tmp', so, 0), ('tmp', so, 1)]
                P.op('act', lambda e, so=so: e.activation(out=junk_f[:], in_=tmp[so][:], func=AF.Square, accum_out=fss[:, so:so + 1]),
                     reads=tk, writes=['junk', ('fss2', so)])
                P.op('act', lambda e, so=so: e.activation(out=fss[:, so:so + 1], in_=fss[:, so:so + 1], func=AF.Sqrt, bias=K.epsc[:], scale=1.0 / D),
                     reads=[('fss2', so), 'epsc'], writes=[('fss2', so)])
                P.op('dve', lambda e, so=so: e.reciprocal(out=frs[:, so:so + 1], in_=fss[:, so:so + 1]), reads=[('fss2', so)], writes=[('frs2', so)])
                P.op('dve', lambda e, so=so: e.scalar_tensor_tensor(out=tmp[so][:], in0=tmp[so][:], scalar=frs[:, so:so + 1], in1=G[:, 1, :],
                                                                    op0=ALU.mult, op1=ALU.mult),
                     reads=tk + [('frs2', so), 'G'], writes=tk)
                P.op('sp', lambda e, so=so, bi=bi, r0=r0: e.dma_start(
                    out=K.out[r0 - HALO + bi * 128:r0 - HALO + (bi + 1) * 128, :], in_=tmp[so][:]),
                    reads=[('tmp', so, 0), ('tmp', so, 1)], dma=('st', so))
            else:
                P.op('sp', lambda e, so=so, bi=bi, d_ap=d_ap, r0=r0: e.dma_start(
                    out=d_ap[r0 + bi * 128:r0 + (bi + 1) * 128, :], in_=tmp[so][:]),
                    reads=[('tmp', so, 0), ('tmp', so, 1)], dma=('st', so))
        if nxt_t is not None:
            norm_T(K, nxt_t[3], xs, l, nxt_t[4], ni, xnT)
    if fuse_final:
        P.op('sp', None, extra=[(P.dsem[('st', so_)], P.dcnt[('st', so_)], 'dma') for so_ in range(2)], signal=False)
    P.barrier()
    st.close()


def final_phase(K, src):
    nc, P = K.nc, K.P
    st = ExitStack()
    Gf = K.sb("Gf", [128, D], stack=st)
    P.op('sp', lambda e: e.dma_start(out=Gf[:], in_=K.fng[None, :].partition_broadcast(128)), writes=['Gf'], dma='Gf')
    xt = [K.sb(f"fx{s}", [128, D], stack=st) for s in range(3)]
    junk = K.sb("fjunk", [128, D], stack=st)
    ss = K.sb("fss", [128, 32], stack=st)
    rs = K.sb("frs", [128, 32], stack=st)
    for bi in range(TOK // 128):
        s = bi % 3
        P.op('sp', lambda e, s=s, bi=bi: e.dma_start(out=xt[s][:], in_=src[HALO + bi * 128:HALO + (bi + 1) * 128, :]),
             writes=[('fx', s)], dma=('fx', s))
        P.op('act', lambda e, s=s, bi=bi: e.activation(out=junk[:], in_=xt[s][:], func=AF.Square,
                                                       accum_out=ss[:, bi:bi + 1]),
             reads=[('fx', s)], writes=['fjunk', ('fss', bi)])
        P.op('act', lambda e, bi=bi: e.activation(out=ss[:, bi:bi + 1], in_=ss[:, bi:bi + 1], func=AF.Sqrt,
                                                  bias=K.epsc[:], scale=1.0 / D),
             reads=[('fss', bi), 'epsc'], writes=[('fss', bi)])
        P.op('dve', lambda e, bi=bi: e.reciprocal(out=rs[:, bi:bi + 1], in_=ss[:, bi:bi + 1]),
             reads=[('fss', bi)], writes=[('frs', bi)])
        P.op('dve', lambda e, s=s, bi=bi: e.scalar_tensor_tensor(
            out=xt[s][:], in0=xt[s][:], scalar=rs[:, bi:bi + 1], in1=Gf[:], op0=ALU.mult, op1=ALU.mult),
            reads=[('fx', s), ('frs', bi), 'Gf'], writes=[('fx', s)])
        P.op('sp', lambda e, s=s, bi=bi: e.dma_start(out=K.out[bi * 128:(bi + 1) * 128, :], in_=xt[s][:]),
             reads=[('fx', s)], writes=['out'], dma=('fo', s))
    P.op('sp', None, extra=[(P.dsem[('fo', s)], P.dcnt[('fo', s)], 'dma') for s in range(3) if ('fo', s) in P.dsem], signal=False)
    st.close()


def host_inputs(inp, names=None):
    x = np.asarray(inp["x"], np.float32)
    maps = []
    ident = np.eye(128, dtype=np.float32)
    f32 = lambda a: np.ascontiguousarray(np.asarray(a, np.float32))
    shared = {k: f32(inp[k]) for k in ("ada_b", "norm_g", "final_norm_g")}
    for l in range(2):
        shared[f"ada_w{l}"] = f32(np.asarray(inp["ada_w"])[l])
        for i in range(2):
            shared[f"wg{l}{i}"] = f32(np.asarray(inp["ffn_w_gate"])[l, i])
            shared[f"wu{l}{i}"] = f32(np.asarray(inp["ffn_w_up"])[l, i])
            shared[f"wd{l}{i}"] = f32(np.asarray(inp["ffn_w_down"])[l, i])
    for k in ("ev_w_in", "ev_w_out", "gm_ws"):
        shared[k] = f32(np.asarray(inp[k])[0])
    for k in ("gm_norm_g", "gm_bs", "attn_sink"):
        shared[k] = f32(inp[k])
    for k in ("od_w_in", "od_w_out", "rg_wa", "rg_wx"):
        shared[k] = f32(np.asarray(inp[k])[0])
    col = lambda v: np.asarray(v, np.float32).reshape(8, 128).T
    od = np.zeros((128, 11, 8), np.float32)
    for kk in range(4):
        od[:, kk] = col(np.asarray(inp["conv_w"])[0, kk])
    od[:, 4] = col(np.asarray(inp["conv_b"])[0])
    for dd in range(2):
        od[:, 5 + dd] = col(np.asarray(inp["rg_ba"])[0, dd])
        od[:, 7 + dd] = col(np.asarray(inp["rg_bx"])[0, dd])
        od[:, 9 + dd] = col(np.asarray(inp["rg_lambda"])[0, dd])
    shared["odcols"] = od
    ii = np.arange(128)
    mprev = np.tile((ii[:, None] >= ii[None, :]).astype(np.float32), (1, 4))
    mnext = np.tile((ii[:, None] <= ii[None, :]).astype(np.float32), (1, 4))
    inv = (10000.0 ** (-np.arange(16, dtype=np.float32) / 16)).astype(np.float32)
    for core in range(4 * x.shape[0]):
        b, q = core // 4, core % 4
        xh = np.zeros((NTH, D), np.float32)
        lo, hi = q * TOK - HALO, (q + 1) * TOK + HALO
        slo, shi = max(lo, 0), min(hi, SEQ)
        xh[slo - lo:shi - lo] = x[b, slo:shi]
        cc = np.stack([np.asarray(inp["c"], np.float32)[b], np.asarray(inp["c_ctx"], np.float32)], 0)
        ccT = np.ascontiguousarray(cc.reshape(2, 8, 128).transpose(2, 1, 0))
        m = {"xh": xh, "ctxb": np.ascontiguousarray(np.asarray(inp["ctx"], np.float32)[b]), "ccT": ccT,
             "ident": ident}
        pos = np.clip(q * TOK - HALO + np.arange(NTH), 0, SEQ - 1)
        ar = (pos // 64).astype(np.float32)[None, :] * inv[:, None]
        ac = (pos % 64).astype(np.float32)[None, :] * inv[:, None]
        C = np.ones((64, NBLK * 128), np.float32)
        Sn = np.zeros((64, NBLK * 128), np.float32)
        C[0:16, :NTH] = np.cos(ar); C[16:32, :NTH] = np.cos(ar); C[32:48, :NTH] = np.cos(ac); C[48:64, :NTH] = np.cos(ac)
        Sn[0:16, :NTH] = -np.sin(ar); Sn[16:32, :NTH] = np.sin(ar); Sn[32:48, :NTH] = -np.sin(ac); Sn[48:64, :NTH] = np.sin(ac)
        m["ropeC"], m["ropeS"] = C, Sn
        sel = np.zeros(24, np.float32)
        if q > 0:
            sel[3 * (q - 1) + 2] = 1.0
        if q < 3:
            sel[3 * (q + 1) + 0] = 1.0
            sel[3 * (q + 1) + 1] = 1.0
        sel[12 + q] = 1.0
        sel[16 + q] = 1.0
        m["selc"] = np.tile(sel[None, :], (128, 1))
        m["amask"] = np.stack([mprev, mnext, mprev * (q != 0), mnext * (q != 3)], 0).astype(np.float32)
        m.update(shared)
        if names is not None:
            m = {k: v for k, v in m.items() if k in names}
        maps.append(m)
    return maps


_NC_CACHE = {}


def kernel(**inputs):
    if "nc" not in _NC_CACHE:
        _NC_CACHE["nc"] = build()
    nc = _NC_CACHE["nc"]
    maps = host_inputs(inputs, LASTNAMES)
    res = run_bass_kernel_spmd(nc, maps, core_ids=list(range(8)))
    out = np.zeros((2, SEQ, D), np.float32)
    for core in range(8):
        b, q = core // 4, core % 4
        out[b, q * TOK:(q + 1) * TOK] = res.results[core]["out"]
    return out


NBLK = 36
TM = 256


def even_mixer_phase(K, src, dst, srcc, dstc):
    nc, P = K.nc, K.P
    st = ExitStack()
    sb = lambda n, s, d=F32: K.sb(n, s, d, stack=st)
    l = 0
    win = sb("win", [128, 8, 1792], BF16)
    wsw = sb("wsw", [128, 8, 640], BF16)
    wout = sb("wout", [128, 8, 1024], BF16)
    kT = sb("kT", [64, 2, NBLK * 128], BF16)
    vv = sb("vv", [128, NBLK, 2, 65], BF16)
    xs = sb("xs", [128, TM // 128, D])
    xnT = sb("xnT", [128, 8, TM], BF16)
    K.junk = sb("junk", [128, D], BF16)
    junk_m = K.junk
    K.ss = sb("ss", [128, 4])
    K.rstd = sb("rstd", [128, 4])
    ct = sb("ct", [64, TM])
    sn = sb("sn", [64, TM])
    t1 = sb("t1", [64, TM])
    t2 = sb("t2", [64, TM])
    identb = sb("identb", [128, 128], BF16)
    P.op('dve', lambda e: e.tensor_copy(out=identb[:], in_=K.identf[:]), reads=['identf'], writes=['identb'])
    wkey = ('wcast', 'ev')
    P.op('sp', lambda e: e.dma_start(out=win[:], in_=K.winb[:, :, :]), reads=[wkey], writes=['win'], dma='win')
    P.op('sp', lambda e: e.dma_start(out=wout[:], in_=K.woutb[:, :, :]), reads=[wkey], writes=['wout'], dma='wout')
    winv = win[:, :, 1024:1664].rearrange("p k (g b e) -> p k g b e", b=2, e=16)
    wswv = wsw[:].rearrange("p k (g b e) -> p k g b e", b=2, e=16)
    for k in range(8):
        for b in range(2):
            P.op('dve', lambda e, k=k, b=b: e.tensor_copy(out=wswv[:, k, :, b, :], in_=winv[:, k, :, 1 - b, :]),
                 reads=['win'], writes=['wsw'])
    P.op('dve', lambda e: e.memset(vv[:, :, :, 64:65], 1.0), writes=['vv1'])

    def kv_tile(s_ap, r0, nb, t, blk0):
        T = nb * 128
        norm_tile(K, None, s_ap, r0, nb, xs, l, t, 1, xnT)
        P.op('sp', lambda e: e.dma_start(out=ct[:, 0:T], in_=K.ropeC[:, blk0 * 128:blk0 * 128 + T]), writes=['ct'], dma='ct')
        P.op('sp', lambda e: e.dma_start(out=sn[:, 0:T], in_=K.ropeS[:, blk0 * 128:blk0 * 128 + T]), writes=['sn'], dma='sn')
        for kvh in range(2):
            for (w_t, c0, bki) in ((win, 1536 + kvh * 64, 2), (wsw, 512 + kvh * 64, 3)):
                for k in range(8):
                    P.op('pe', lambda e, w_t=w_t, c0=c0, bki=bki, k=k: e.matmul(
                        K.bank[bki][0:64, 0:T], lhsT=w_t[:, k, c0:c0 + 64], rhs=xnT[:, k, 0:T], start=(k == 0), stop=(k == 7)),
                        reads=['win', 'wsw', 'xnT'], writes=[('bank', bki)], signal=(k == 7))
            P.op('dve', lambda e: e.tensor_tensor(out=t1[:, 0:T], in0=K.bank[2][0:64, 0:T], in1=ct[:, 0:T], op=ALU.mult),
                 reads=[('bank', 2), 'ct'], writes=['t1'])
            P.op('dve', lambda e: e.tensor_tensor(out=t2[:, 0:T], in0=K.bank[3][0:64, 0:T], in1=sn[:, 0:T], op=ALU.mult),
                 reads=[('bank', 3), 'sn'], writes=['t2'])
            P.op('dve', lambda e, kvh=kvh: e.tensor_tensor(out=kT[:, kvh, blk0 * 128:blk0 * 128 + T], in0=t1[:, 0:T], in1=t2[:, 0:T], op=ALU.add),
                 reads=['t1', 't2'], writes=['kT'])
        for bi in range(nb):
            bki = 4 + bi % 2
            for k in range(8):
                P.op('pe', lambda e, bki=bki, k=k, bi=bi: e.matmul(
                    K.bank[bki][:, 0:128], lhsT=xnT[:, k, bi * 128:(bi + 1) * 128], rhs=win[:, k, 1664:1792],
                    start=(k == 0), stop=(k == 7)), reads=['win', 'xnT'], writes=[('bank', bki)], signal=(k == 7))
            P.op('act', lambda e, bki=bki, bi=bi: e.activation(
                out=vv[:, blk0 + bi, :, 0:64], in_=K.bank[bki][:, 0:128].rearrange("p (a b) -> p a b", a=2), func=AF.Copy),
                reads=[('bank', bki)], writes=['vv'])

    r = 0
    while r < NTH:
        nb = min(TM, NTH - r) // 128
        kv_tile(src, r, nb, 0, r // 128)
        r += nb * 128
    kv_tile(srcc, 0, 2, 1, 34)

    uT = sb("uT", [128, 4, TM], BF16)
    qT = sb("qT", [64, 8, TM], BF16)
    zg = [sb(f"zg{s}", [128, 512]) for s in range(2)]
    zn = sb("zn", [128, TM // 128, 512], BF16)
    zss = sb("zss", [128, 2])
    zr = sb("zr", [128, 2])
    mixT = sb("mixT", [128, 8, TM], BF16)
    pT = [sb(f"pT{s}", [128, 512], BF16) for s in range(10)]
    btok = sb("btok", [128, 512], BF16)
    den = sb("den", [128, 4])
    rec = sb("rec", [128, 4])
    masks = sb("masks", [128, 4, 512])
    bsB = sb("bsB", [128, 4, TM])
    gmg = sb("gmg", [128, 512])
    G5 = sb("G5", [128, 2, D])
    esink = sb("esink", [128, 8])
    wsT = sb("wsT", [128, 4, 128], BF16)
    wsl = sb("wsl", [128, 4, 128])
    tmpa = sb("tmpa", [128, TM])
    xr = [sb(f"xr{s}", [128, D]) for s in range(2)]
    tmp = [sb(f"tmp{s}", [128, D]) for s in range(2)]
    for m in range(4):
        P.op('sp', lambda e, m=m: e.dma_start(out=masks[:, m, :], in_=K.amask[m]), writes=['masks'], dma='masks')
    for rr in range(TM // 128):
        P.op('sp', lambda e, rr=rr: e.dma_start(out=bsB[:, :, rr * 128:(rr + 1) * 128],
                                               in_=K.gm_bs[0:1, :, :].partition_broadcast(128)), writes=['bsB'], dma='bsB')
    P.op('sp', lambda e: e.dma_start(out=gmg[:], in_=K.gm_norm_g[0:1, :].partition_broadcast(128)), writes=['gmg'], dma='gmg')
    P.op('sp', lambda e: e.dma_start(out=esink[:], in_=K.attn_sink[0:1, :].partition_broadcast(128)), writes=['esink'], dma='esink')
    P.op('act', lambda e: e.activation(out=esink[:], in_=esink[:], func=AF.Exp), reads=['esink'], writes=['esink'])
    for t in range(2):
        if K.nomod:
            P.op('dve', lambda e, t=t: e.memset(G5[:, t, :], 1.0), writes=['G5'])
        else:
            P.op('sp', lambda e, t=t: e.dma_start(out=G5[:, t, :], in_=K.modrows[l, t:t + 1, 5 * D:6 * D].partition_broadcast(128)),
                 reads=[('modrows', l)], writes=['G5'], dma='G5')
    P.op('sp', lambda e: e.dma_start(out=wsl[:], in_=K.gm_ws.rearrange("g p q -> p g q")), writes=['wsl'], dma='wsl')
    for g in range(4):
        P.op('pe', lambda e, g=g: e.transpose(out=K.bank[2][:, g * 128:(g + 1) * 128], in_=wsl[:, g, :], identity=K.identf[:]),
             reads=['wsl', 'identf'], writes=[('bank', 2)], signal=(g == 3))
    P.op('dve', lambda e: e.tensor_copy(out=wsT[:], in_=K.bank[2][:, :].rearrange("p (g q) -> p g q", g=4)),
         reads=[('bank', 2)], writes=['wsT'])
    bankT = K.bank[3][:, :].bitcast(BF16)
    pcnt = [0]
    on = [0]

    def mix_prefetch(s_ap, d_ap, r0, nb, t, blk0):
        T = nb * 128
        norm_load(K, s_ap, r0, nb, xs)
        P.op('sp', lambda e: e.dma_start(out=ct[:, 0:T], in_=K.ropeC[:, blk0 * 128:blk0 * 128 + T]), writes=['ct'], dma='ct')
        P.op('sp', lambda e: e.dma_start(out=sn[:, 0:T], in_=K.ropeS[:, blk0 * 128:blk0 * 128 + T]), writes=['sn'], dma='sn')

    def mix_tile(s_ap, d_ap, r0, nb, t, blk0, nxt=None):
        T = nb * 128
        norm_stats(K, nb, xs)
        norm_T(K, nb, xs, l, t, 1, xnT)
        for c in range(4):
            bki = 2 + c % 2
            for k in range(8):
                P.op('pe', lambda e, bki=bki, k=k, c=c: e.matmul(
                    K.bank[bki][:, 0:T], lhsT=win[:, k, c * 128:(c + 1) * 128], rhs=xnT[:, k, 0:T], start=(k == 0), stop=(k == 7)),
                    reads=['win', 'xnT'], writes=[('bank', bki)], signal=(k == 7))
            P.op('act', lambda e, bki=bki, c=c: e.activation(out=uT[:, c, 0:T], in_=K.bank[bki][:, 0:T], func=AF.Gelu_apprx_tanh),
                 reads=[('bank', bki)], writes=['uT'])
        for h in range(8):
            for (w_t, c0, bki) in ((win, 1024 + h * 64, 2), (wsw, h * 64, 3)):
                for k in range(8):
                    P.op('pe', lambda e, w_t=w_t, c0=c0, bki=bki, k=k: e.matmul(
                        K.bank[bki][0:64, 0:T], lhsT=w_t[:, k, c0:c0 + 64], rhs=xnT[:, k, 0:T], start=(k == 0), stop=(k == 7)),
                        reads=['win', 'wsw', 'xnT'], writes=[('bank', bki)], signal=(k == 7))
            P.op('dve', lambda e: e.tensor_tensor(out=t1[:, 0:T], in0=K.bank[2][0:64, 0:T], in1=ct[:, 0:T], op=ALU.mult),
                 reads=[('bank', 2), 'ct'], writes=['t1'])
            P.op('dve', lambda e: e.tensor_tensor(out=t2[:, 0:T], in0=K.bank[3][0:64, 0:T], in1=sn[:, 0:T], op=ALU.mult),
                 reads=[('bank', 3), 'sn'], writes=['t2'])
            P.op('dve', lambda e, h=h: e.tensor_tensor(out=qT[:, h, 0:T], in0=t1[:, 0:T], in1=t2[:, 0:T], op=ALU.add),
                 reads=['t1', 't2'], writes=['qT'])
        for bi in range(nb):
            bki = 4 + bi % 2
            for k in range(8):
                P.op('pe', lambda e, bki=bki, k=k, bi=bi: e.matmul(
                    K.bank[bki][:, :], lhsT=xnT[:, k, bi * 128:(bi + 1) * 128], rhs=win[:, k, 512:1024],
                    start=(k == 0), stop=(k == 7)), reads=['win', 'xnT'], writes=[('bank', bki)], signal=(k == 7))
            zs = bi % 2
            P.op('act', lambda e, bki=bki, zs=zs: e.activation(out=zg[zs][:], in_=K.bank[bki][:, :], func=AF.Gelu_apprx_tanh),
                 reads=[('bank', bki)], writes=[('zg', zs)])
            P.op('act', lambda e, zs=zs: e.activation(out=junk_m[:, 0:512], in_=zg[zs][:], func=AF.Square, accum_out=zss[:, zs:zs + 1]),
                 reads=[('zg', zs)], writes=['junk', ('zss', zs)])
            P.op('act', lambda e, zs=zs: e.activation(out=zss[:, zs:zs + 1], in_=zss[:, zs:zs + 1], func=AF.Sqrt, bias=K.epsc[:], scale=1.0 / 512),
                 reads=[('zss', zs), 'epsc'], writes=[('zss', zs)])
            P.op('dve', lambda e, zs=zs: e.reciprocal(out=zr[:, zs:zs + 1], in_=zss[:, zs:zs + 1]), reads=[('zss', zs)], writes=[('zr', zs)])
            P.op('dve', lambda e, zs=zs, bi=bi: e.scalar_tensor_tensor(out=zn[:, bi, :], in0=zg[zs][:], scalar=zr[:, zs:zs + 1], in1=gmg[:],
                                                                       op0=ALU.mult, op1=ALU.mult),
                 reads=[('zg', zs), ('zr', zs), 'gmg'], writes=['zn'])
        for g in range(4):
            bki = 2 + g % 2
            for bi in range(nb):
                P.op('pe', lambda e, bki=bki, g=g, bi=bi: e.matmul(
                    K.bank[bki][:, bi * 128:(bi + 1) * 128], lhsT=zn[:, bi, g * 128:(g + 1) * 128], rhs=wsT[:, g, :],
                    start=True, stop=True), reads=['zn', 'wsT'], writes=[('bank', bki)], signal=(bi == nb - 1))
            P.op('dve', lambda e, bki=bki, g=g: e.tensor_tensor(out=tmpa[:, 0:T], in0=K.bank[bki][:, 0:T], in1=bsB[:, g, 0:T], op=ALU.add),
                 reads=[('bank', bki), 'bsB'], writes=['tmpa'])
            P.op('dve', lambda e, g=g: e.tensor_tensor(out=mixT[:, g, 0:T], in0=tmpa[:, 0:T], in1=uT[:, g, 0:T], op=ALU.mult),
                 reads=['tmpa', 'uT'], writes=['mixT'])
        for bi in range(nb):
            blk = blk0 + bi
            if t == 0:
                kbs = [(blk - 1, 2 if blk == 1 else 0), (blk, None), (blk + 1, 3 if blk == TOK // 128 else 1), (34, None), (35, None)]
            else:
                kbs = [(34, None), (35, None)]
            for kvh in range(2):
                slots = []
                for (kb, mk) in kbs:
                    sl = pcnt[0] % 10
                    pcnt[0] += 1
                    slots.append(sl)
                    bki = 4 + sl % 2
                    P.op('pe', lambda e, bki=bki, kb=kb, kvh=kvh, bi=bi: e.matmul(
                        K.bank[bki][:, :], lhsT=kT[:, kvh, kb * 128:(kb + 1) * 128],
                        rhs=qT[:, 4 * kvh:4 * kvh + 4, bi * 128:(bi + 1) * 128], start=True, stop=True),
                        reads=['kT', 'qT'], writes=[('bank', bki)])
                    P.op('act', lambda e, bki=bki, sl=sl: e.activation(out=pT[sl][:], in_=K.bank[bki][:, :], func=AF.Exp, scale=0.125),
                         reads=[('bank', bki)], writes=[('pT', sl)])
                    if mk is not None:
                        P.op('dve', lambda e, sl=sl, mk=mk: e.tensor_tensor(out=pT[sl][:], in0=pT[sl][:], in1=masks[:, mk, :], op=ALU.mult),
                             reads=[('pT', sl), 'masks'], writes=[('pT', sl)])
                bko = 6 + kvh
                for h in range(4):
                    for ki, (kb, mk) in enumerate(kbs):
                        P.op('pe', lambda e, bko=bko, h=h, ki=ki, kb=kb, kvh=kvh, sl=slots[ki]: e.matmul(
                            K.bank[bko][:, h * 65:(h + 1) * 65], lhsT=pT[sl][:, h * 128:(h + 1) * 128], rhs=vv[:, kb, kvh, :],
                            start=(ki == 0), stop=(ki == len(kbs) - 1)),
                            reads=[('pT', slots[ki]), 'vv', 'vv1'], writes=[('bank', bko)], signal=(h == 3 and ki == len(kbs) - 1))
                ov = K.bank[bko][:, 0:260].rearrange("p (h c) -> p h c", c=65)
                P.op('dve', lambda e, ov=ov, kvh=kvh: e.tensor_tensor(out=den[:], in0=ov[:, :, 64], in1=esink[:, 4 * kvh:4 * kvh + 4], op=ALU.add),
                     reads=[('bank', bko), 'esink'], writes=['den'])
                P.op('dve', lambda e: e.reciprocal(out=rec[:], in_=den[:]), reads=['den'], writes=['rec'])
                for h in range(4):
                    P.op('act', lambda e, ov=ov, h=h, kvh=kvh: e.activation(
                        out=btok[:, (4 * kvh + h) * 64:(4 * kvh + h + 1) * 64], in_=ov[:, h, 0:64], func=AF.Copy, scale=rec[:, h:h + 1]),
                        reads=[('bank', bko), 'rec'], writes=['btok'])
            for c in range(4):
                P.op('pe', lambda e, c=c: e.transpose(out=bankT[:, c * 128:(c + 1) * 128], in_=btok[:, c * 128:(c + 1) * 128], identity=identb[:]),
                     reads=['btok', 'identb'], writes=[('bank', 3)], signal=(c == 3))
            P.op('act', lambda e, bi=bi: e.activation(out=mixT[:, 4:8, bi * 128:(bi + 1) * 128],
                                                      in_=bankT[:, 0:512].rearrange("p (c q) -> p c q", c=4), func=AF.Copy),
                 reads=[('bank', 3)], writes=['mixT'])
        if nxt is not None:
            mix_prefetch(*nxt)
        for bi in range(nb):
            so = on[0] % 2
            on[0] += 1
            P.op('sp', lambda e, so=so, bi=bi: e.dma_start(out=xr[so][:], in_=s_ap[r0 + bi * 128:r0 + (bi + 1) * 128, :]),
                 writes=[('xr', so)], dma=('xr', so))
            for fh in range(2):
                bki = fh
                for kc in range(8):
                    P.op('pe', lambda e, bki=bki, kc=kc, bi=bi, fh=fh: e.matmul(
                        K.bank[bki][:, :], lhsT=mixT[:, kc, bi * 128:(bi + 1) * 128], rhs=wout[:, kc, fh * 512:(fh + 1) * 512],
                        start=(kc == 0), stop=(kc == 7)), reads=['mixT', 'wout'], writes=[('bank', bki)], signal=(kc == 7))
                P.op('dve', lambda e, bki=bki, so=so, fh=fh: e.tensor_tensor(
                    out=tmp[so][:, fh * 512:(fh + 1) * 512], in0=K.bank[bki][:, :], in1=G5[:, t, fh * 512:(fh + 1) * 512], op=ALU.mult),
                    reads=[('bank', bki), 'G5'], writes=[('tmp', so, fh)])
                P.op('dve', lambda e, so=so, fh=fh: e.tensor_tensor(
                    out=tmp[so][:, fh * 512:(fh + 1) * 512], in0=tmp[so][:, fh * 512:(fh + 1) * 512], in1=xr[so][:, fh * 512:(fh + 1) * 512], op=ALU.add),
                    reads=[('tmp', so, fh), ('xr', so)], writes=[('tmp', so, fh)])
            P.op('sp', lambda e, so=so, bi=bi: e.dma_start(out=d_ap[r0 + bi * 128:r0 + (bi + 1) * 128, :], in_=tmp[so][:]),
                 reads=[('tmp', so, 0), ('tmp', so, 1)], dma=('st', so))

    mtiles = [(src, dst, HALO + s * TM, TM // 128, 0, (HALO + s * TM) // 128) for s in range(TOK // TM)]
    mtiles.append((srcc, dstc, 0, 2, 1, 34))
    mix_prefetch(*mtiles[0])
    for mi, mt in enumerate(mtiles):
        mix_tile(*mt, nxt=(mtiles[mi + 1] if mi + 1 < len(mtiles) else None))
    P.barrier()
    st.close()


TO = 256
RG = [[0, 1, 2, 3], [4, 5, 6, 7]]


def odd_mixer_phase(K, src, dst, srcc):
    nc, P = K.nc, K.P
    st = ExitStack()
    stA = ExitStack()
    sb = lambda n, s, d=F32: K.sb(n, s, d, stack=st)
    sbA = lambda n, s, d=F32: K.sb(n, s, d, stack=stA)
    l = 1
    NTL = TOK // TO
    TA = 512
    NTA = TOK // TA
    owout = sb("owout", [128, 8, 1024], BF16)
    wa = sb("wa", [128, 16, 256], BF16)
    wx = sb("wx", [128, 16, 256], BF16)
    oc_ = sb("odcols", [128, 11, 8])
    cl = sb("cl", [128, 2, 8])
    selc = sb("selc", [128, 24])
    owin = sbA("owin", [128, 8, 2048], BF16)
    xs = sbA("xs", [128, TA // 128, D])
    xnT = sbA("xnT", [128, 8, TA], BF16)
    K.junk = sbA("junk", [128, D], BF16)
    K.ss = sbA("ss", [128, 4])
    K.rstd = sbA("rstd", [128, 4])
    wkey = ('wcast', 'od')
    P.op('sp', lambda e: e.dma_start(out=owin[:], in_=K.owinb[:, :, :]), reads=[wkey], writes=['owin'], dma='owin')
    P.op('sp', lambda e: e.dma_start(out=owout[:], in_=K.owoutb[:, :, :]), reads=[wkey], writes=['owout'], dma='owout')
    P.op('sp', lambda e: e.dma_start(out=wa[:], in_=K.wab[:, :, :]), reads=[wkey], writes=['wa'], dma='wa')
    P.op('sp', lambda e: e.dma_start(out=wx[:], in_=K.wxb[:, :, :]), reads=[wkey], writes=['wx'], dma='wx')
    P.op('sp', lambda e: e.dma_start(out=oc_[:], in_=K.odcols[:, :, :]), writes=['odcols'], dma='odcols')
    P.op('act', lambda e: e.activation(out=cl[:], in_=oc_[:, 9:11, :], func=AF.Exp, scale=-1.0), reads=['odcols'], writes=['cl'])
    P.op('dve', lambda e: e.tensor_scalar(out=cl[:], in0=cl[:], scalar1=1.0, scalar2=None, op0=ALU.add), reads=['cl'], writes=['cl'])
    P.op('act', lambda e: e.activation(out=cl[:], in_=cl[:], func=AF.Ln), reads=['cl'], writes=['cl'])
    P.op('dve', lambda e: e.tensor_scalar(out=cl[:], in0=cl[:], scalar1=-8.0, scalar2=None, op0=ALU.mult), reads=['cl'], writes=['cl'])

    gT = sbA("gT", [128, 8, TA], BF16)
    xxT = sbA("xxT", [128, 8, TA])
    zero3 = sbA("zero3", [128, 8, 3])
    P.op('dve', lambda e: e.memset(zero3[:], 0.0), writes=['zero3'])
    P.op('sp', lambda e: e.dma_start(out=K.xxc[:, :, 0:1], in_=zero3[:, :, 0:1], allow_slow_non_contiguous=True), reads=['zero3'], dma='z0')
    P.op('sp', lambda e: e.dma_start(out=K.xxc[:, :, LCTX + 1:LCTX + 3], in_=zero3[:, :, 0:2]), reads=['zero3'], dma='z1')

    def proj_tile(s_ap, r0, nb, t, ti):
        T = nb * 128
        norm_tile(K, None, s_ap, r0, nb, xs, l, t, 1, xnT)
        for c in range(8):
            if t == 0:
                bki = 2 + c % 2
                for k in range(8):
                    P.op('pe', lambda e, bki=bki, k=k, c=c: e.matmul(
                        K.bank[bki][:, 0:T], lhsT=owin[:, k, c * 128:(c + 1) * 128], rhs=xnT[:, k, 0:T], start=(k == 0), stop=(k == 7)),
                        reads=['owin', 'xnT'], writes=[('bank', bki)], signal=(k == 7))
                P.op('act', lambda e, bki=bki, c=c: e.activation(out=gT[:, c, 0:T], in_=K.bank[bki][:, 0:T], func=AF.Gelu_apprx_tanh),
                     reads=[('bank', bki)], writes=['gT'])
            bki = 4 + c % 2
            for k in range(8):
                P.op('pe', lambda e, bki=bki, k=k, c=c: e.matmul(
                    K.bank[bki][:, 0:T], lhsT=owin[:, k, 1024 + c * 128:1024 + (c + 1) * 128], rhs=xnT[:, k, 0:T], start=(k == 0), stop=(k == 7)),
                    reads=['owin', 'xnT'], writes=[('bank', bki)], signal=(k == 7))
            P.op('dve', lambda e, bki=bki, c=c: e.tensor_copy(out=xxT[:, c, 0:T], in_=K.bank[bki][:, 0:T]),
                 reads=[('bank', bki)], writes=['xxT'])
        if t == 0:
            P.op('sp', lambda e: e.dma_start(out=K.gg[:, :, ti * TA:ti * TA + T], in_=gT[:, :, 0:T]), reads=['gT'], dma='ggst')
            P.op('sp', lambda e: e.dma_start(out=K.xxs[:, :, 1 + ti * TA:1 + ti * TA + T], in_=xxT[:, :, 0:T]), reads=['xxT'], dma='xxst')
            if ti == 0:
                for j in range(2):
                    P.op('sp', lambda e, j=j: e.dma_start(out=K.bnc1[j].rearrange("(c p) -> p c", p=128), in_=xxT[:, :, j],
                                                         allow_slow_non_contiguous=True), reads=['xxT'], writes=['bnc1'], dma='bnc1')
            if ti == NTA - 1:
                P.op('sp', lambda e: e.dma_start(out=K.bnc1[2].rearrange("(c p) -> p c", p=128), in_=xxT[:, :, T - 1],
                                                 allow_slow_non_contiguous=True), reads=['xxT'], writes=['bnc1'], dma='bnc1')
        else:
            P.op('sp', lambda e: e.dma_start(out=K.xxc[:, :, 1:1 + T], in_=xxT[:, :, 0:T]), reads=['xxT'], dma='xxst')

    for ti in range(NTA):
        proj_tile(src, HALO + ti * TA, TA // 128, 0, ti)
    proj_tile(srcc, 0, LCTX // 128, 1, 0)

    P.op('pool', lambda e: e.collective_compute("AllGather", ALU.bypass, replica_groups=RG, ins=[K.bnc1[:, :].opt()], outs=[K.gath1[:, :].opt()]),
         reads=['bnc1'], writes=['gath1'], dma='ag1', inc=1)
    g1 = sbA("g1", [128, 12, 8])
    for r in range(12):
        P.op('sp', lambda e, r=r: e.dma_start(out=g1[:, r, :], in_=K.gath1[r].rearrange("(c p) -> p c", p=128), allow_slow_non_contiguous=True),
             reads=['gath1'], writes=['g1'], dma='g1')
    P.op('sp', lambda e: e.dma_start(out=selc[:], in_=K.selc[:, :]), writes=['selc'], dma='selc')
    halo = sbA("halo", [128, 8, 3])
    P.op('dve', lambda e: e.memset(halo[:], 0.0), writes=['halo'])
    for j, rows in ((0, [2, 5, 8, 11]), (1, [0, 3, 6, 9]), (2, [1, 4, 7, 10])):
        for r in rows:
            P.op('dve', lambda e, j=j, r=r: e.scalar_tensor_tensor(out=halo[:, :, j], in0=g1[:, r, :], scalar=selc[:, r:r + 1], in1=halo[:, :, j],
                                                                  op0=ALU.mult, op1=ALU.add), reads=['g1', 'selc', 'halo'], writes=['halo'])
    P.op('sp', lambda e: e.dma_start(out=K.xxs[:, :, 0:1], in_=halo[:, :, 0:1], allow_slow_non_contiguous=True), reads=['halo', 'xxT'], dma='h0')
    P.op('sp', lambda e: e.dma_start(out=K.xxs[:, :, TOK + 1:TOK + 3], in_=halo[:, :, 1:3]), reads=['halo', 'xxT'], dma='h1')
    P.barrier()
    stA.close()

    xin = sb("xin", [128, 8, TO + 3])
    xc = sb("xc", [128, 8, TO])
    xcb = sb("xcb", [128, 8, TO], BF16)
    r_ts = [sb(f"r_t{i}", [128, 8, TO]) for i in range(2)]
    i_ts = [sb(f"i_t{i}", [128, 8, TO]) for i in range(2)]
    a_ts = [sb(f"a_t{i}", [128, 8, TO]) for i in range(2)]
    b_ts = [sb(f"b_t{i}", [128, 8, TO]) for i in range(2)]
    h_ts = [sb(f"h_t{i}", [128, 8, TO]) for i in range(2)]
    h_t = h_ts[0]
    rsum = sb("rsum", [128, 2, NTL + 1, 8])
    Hl = sb("Hl", [128, 2, NTL + 1, 8])
    Al = sb("Al", [128, 2, NTL + 1, 8])
    state = sb("state", [128, 2, 8])
    one1 = sb("one1", [128, 1])
    P.op('dve', lambda e: e.memset(one1[:], 1.0), writes=['one1'])
    P.op('dve', lambda e: e.memset(rsum[:], 0.0), writes=['rsum'])

    def conv_tile(xsrc, c0, T, store):
        P.op('sp', lambda e: e.dma_start(out=xin[:, :, 0:T + 3], in_=xsrc[:, :, c0:c0 + T + 3]), writes=['xin'], dma='xin')
        for c in range(8):
            P.op('dve', lambda e, c=c: e.tensor_scalar(out=xc[:, c, 0:T], in0=xin[:, c, 0:T], scalar1=oc_[:, 0, c:c + 1],
                                                       scalar2=oc_[:, 4, c:c + 1], op0=ALU.mult, op1=ALU.add),
                 reads=['xin', 'odcols'], writes=[('xc', c)])
            for k in range(1, 4):
                P.op('dve', lambda e, c=c, k=k: e.scalar_tensor_tensor(out=xc[:, c, 0:T], in0=xin[:, c, k:k + T], scalar=oc_[:, k, c:c + 1],
                                                                      in1=xc[:, c, 0:T], op0=ALU.mult, op1=ALU.add),
                     reads=['xin', 'odcols', ('xc', c)], writes=[('xc', c)])
            P.op('pool', lambda e, c=c: e.tensor_copy(out=xcb[:, c, 0:T], in_=xc[:, c, 0:T]), reads=[('xc', c)], writes=[('xcb', c)])
        if store is not None:
            P.op('sp', lambda e: e.dma_start(out=store, in_=xc[:, :, 0:T]), reads=[('xc', c) for c in range(8)], dma='xcst')

    def load_xc(xsrc, T):
        P.op('sp', lambda e: e.dma_start(out=xc[:, :, 0:T], in_=xsrc), writes=[('xc', c) for c in range(8)], dma='xcld')
        for c in range(8):
            P.op('act', lambda e, c=c: e.activation(out=xcb[:, c, 0:T], in_=xc[:, c, 0:T], func=AF.Copy), reads=[('xc', c)], writes=[('xcb', c)])

    def gates_scan(d, T, ti, init_state):
        a_t, b_t = a_ts[d], b_ts[d]
        r_t, i_t = r_ts[d], i_ts[d]
        for c in range(8):
            hd, oc = c // 2, c % 2
            for (wt, bki, dest, brow_i, acc) in ((wa, 2 + c % 2, r_t, 5 + d, True), (wx, 4 + c % 2, i_t, 7 + d, False)):
                for kc in range(2):
                    P.op('pe', lambda e, wt=wt, bki=bki, kc=kc, hd=hd, oc=oc: e.matmul(
                        K.bank[bki][:, 0:T], lhsT=wt[:, (d * 4 + hd) * 2 + kc, oc * 128:(oc + 1) * 128], rhs=xcb[:, 2 * hd + kc, 0:T],
                        start=(kc == 0), stop=(kc == 1)),
                        reads=['wa', 'wx', ('xcb', 2 * hd + kc)], writes=[('bank', bki)], signal=(kc == 1))
                if acc:
                    P.op('act', lambda e, bki=bki, dest=dest, c=c, brow_i=brow_i: e.activation(
                        out=dest[:, c, 0:T], in_=K.bank[bki][:, 0:T], func=AF.Sigmoid, bias=oc_[:, brow_i, c:c + 1],
                        accum_out=rsum[:, d, ti, c:c + 1]),
                        reads=[('bank', bki), 'odcols'], writes=[('r_t', d, c), 'rsum'])
                else:
                    P.op('act', lambda e, bki=bki, dest=dest, c=c, brow_i=brow_i: e.activation(
                        out=dest[:, c, 0:T], in_=K.bank[bki][:, 0:T], func=AF.Sigmoid, bias=oc_[:, brow_i, c:c + 1]),
                        reads=[('bank', bki), 'odcols'], writes=[('i_t', d, c)])
        for c in range(8):
            P.op('act', lambda e, c=c: e.activation(out=a_t[:, c, 0:T], in_=r_t[:, c, 0:T], func=AF.Exp, scale=cl[:, d, c:c + 1]),
                 reads=[('r_t', d, c), 'cl'], writes=[('a_t', d, c)])
            P.op('pool', lambda e, c=c: e.tensor_tensor(out=r_t[:, c, 0:T], in0=a_t[:, c, 0:T], in1=a_t[:, c, 0:T], op=ALU.mult),
                 reads=[('a_t', d, c)], writes=[('r_t', d, c)])
        for c in range(8):
            P.op('act', lambda e, c=c: e.activation(out=r_t[:, c, 0:T], in_=r_t[:, c, 0:T], func=AF.Sqrt, bias=one1[:], scale=-1.0),
                 reads=[('r_t', d, c), 'one1'], writes=[('r_t', d, c)])
            P.op('pool', lambda e, c=c: e.tensor_tensor(out=i_t[:, c, 0:T], in0=i_t[:, c, 0:T], in1=r_t[:, c, 0:T], op=ALU.mult),
                 reads=[('i_t', d, c), ('r_t', d, c)], writes=[('i_t', d, c)])
            P.op('pool', lambda e, c=c: e.tensor_tensor(out=b_t[:, c, 0:T], in0=i_t[:, c, 0:T], in1=xc[:, c, 0:T], op=ALU.mult),
                 reads=[('i_t', d, c), ('xc', c)], writes=[('b_t', d, c)])
            init = 0.0 if init_state is None else init_state[:, d, c:c + 1]
            rk = [('a_t', d, c), ('b_t', d, c)] + ([] if init_state is None else ['state'])
            if d == 0:
                P.op('dve', lambda e, c=c, init=init: e.tensor_tensor_scan(out=h_t[:, c, 0:T], data0=a_t[:, c, 0:T], data1=b_t[:, c, 0:T],
                                                                           initial=init, op0=ALU.mult, op1=ALU.add),
                     reads=rk, writes=[('h_t', c)])
            else:
                P.op('dve', lambda e, c=c, init=init: e.tensor_tensor_scan(out=h_t[:, c, T - 1::-1] if False else h_t[:, c, 0:T][:, ::-1],
                                                                           data0=a_t[:, c, 0:T][:, ::-1], data1=b_t[:, c, 0:T][:, ::-1],
                                                                           initial=init, op0=ALU.mult, op1=ALU.add),
                     reads=rk, writes=[('h_t', c)])

    hkeys = [('h_t', c) for c in range(8)]
    for ti in range(NTL + 1):
        ctxt = ti == NTL
        T = LCTX if ctxt else TO
        if ctxt:
            conv_tile(K.xxc, 0, T, None)
        else:
            conv_tile(K.xxs, ti * TO, T, None)
        for d in range(2):
            gates_scan(d, T, ti, None)
            if not ctxt:
                P.op('sp', lambda e, d=d, ti=ti: e.dma_start(out=K.ad[d][:, :, ti * TO:(ti + 1) * TO], in_=a_ts[d][:, :, :]),
                     reads=[('a_t', d, c) for c in range(8)], dma=('ast', d))
                P.op('sp', lambda e, d=d, ti=ti: e.dma_start(out=K.bd[d][:, :, ti * TO:(ti + 1) * TO], in_=b_ts[d][:, :, :]),
                     reads=[('b_t', d, c) for c in range(8)], dma=('bst', d))
            col = T - 1 if d == 0 else 0
            P.op('dve', lambda e, d=d, ti=ti, col=col: e.tensor_copy(out=Hl[:, d, ti, :], in_=h_t[:, :, col]), reads=hkeys, writes=['Hl'])
    for d in range(2):
        for ti in range(NTL + 1):
            P.op('dve', lambda e, d=d, ti=ti: e.tensor_tensor(out=Al[:, d, ti, :], in0=rsum[:, d, ti, :], in1=cl[:, d, :], op=ALU.mult),
                 reads=['rsum', 'cl'], writes=['Al'])
    P.op('act', lambda e: e.activation(out=Al[:], in_=Al[:], func=AF.Exp), reads=['Al'], writes=['Al'])
    summ = sb("summ", [128, 4, 8])
    P.op('dve', lambda e: e.memset(summ[:], 0.0), writes=['summ'])
    P.op('dve', lambda e: e.memset(summ[:, 0, :], 1.0), reads=['summ'], writes=['summ'])
    P.op('dve', lambda e: e.memset(summ[:, 2, :], 1.0), reads=['summ'], writes=['summ'])
    for ti in range(NTL):
        P.op('dve', lambda e, ti=ti: e.tensor_tensor(out=summ[:, 1, :], in0=summ[:, 1, :], in1=Al[:, 0, ti, :], op=ALU.mult), reads=['summ', 'Al'], writes=['summ'])
        P.op('dve', lambda e, ti=ti: e.tensor_tensor(out=summ[:, 1, :], in0=summ[:, 1, :], in1=Hl[:, 0, ti, :], op=ALU.add), reads=['summ', 'Hl'], writes=['summ'])
        P.op('dve', lambda e, ti=ti: e.tensor_tensor(out=summ[:, 0, :], in0=summ[:, 0, :], in1=Al[:, 0, ti, :], op=ALU.mult), reads=['summ', 'Al'], writes=['summ'])
    for ti in range(NTL - 1, -1, -1):
        P.op('dve', lambda e, ti=ti: e.tensor_tensor(out=summ[:, 3, :], in0=summ[:, 3, :], in1=Al[:, 1, ti, :], op=ALU.mult), reads=['summ', 'Al'], writes=['summ'])
        P.op('dve', lambda e, ti=ti: e.tensor_tensor(out=summ[:, 3, :], in0=summ[:, 3, :], in1=Hl[:, 1, ti, :], op=ALU.add), reads=['summ', 'Hl'], writes=['summ'])
        P.op('dve', lambda e, ti=ti: e.tensor_tensor(out=summ[:, 2, :], in0=summ[:, 2, :], in1=Al[:, 1, ti, :], op=ALU.mult), reads=['summ', 'Al'], writes=['summ'])
    for qi in range(4):
        P.op('sp', lambda e, qi=qi: e.dma_start(out=K.bnc2[qi].rearrange("(c p) -> p c", p=128), in_=summ[:, qi, :], allow_slow_non_contiguous=True),
             reads=['summ'], writes=['bnc2'], dma='bnc2')
    P.op('pool', lambda e: e.collective_compute("AllGather", ALU.bypass, replica_groups=RG, ins=[K.bnc2[:, :].opt()], outs=[K.gath2[:, :].opt()]),
         reads=['bnc2'], writes=['gath2'], dma='ag2', inc=1)
    g2 = sb("g2", [128, 16, 8])
    for r in range(16):
        P.op('sp', lambda e, r=r: e.dma_start(out=g2[:, r, :], in_=K.gath2[r].rearrange("(c p) -> p c", p=128), allow_slow_non_contiguous=True),
             reads=['gath2'], writes=['g2'], dma='g2')
    Sf = sb("Sf", [128, 4, 8])
    Tb = sb("Tb", [128, 4, 8])
    P.op('dve', lambda e: e.tensor_copy(out=Sf[:, 0, :], in_=Hl[:, 0, NTL, :]), reads=['Hl'], writes=['Sf'])
    for r in range(3):
        P.op('dve', lambda e, r=r: e.tensor_tensor(out=Sf[:, r + 1, :], in0=Sf[:, r, :], in1=g2[:, 4 * r + 0, :], op=ALU.mult), reads=['Sf', 'g2'], writes=['Sf'])
        P.op('dve', lambda e, r=r: e.tensor_tensor(out=Sf[:, r + 1, :], in0=Sf[:, r + 1, :], in1=g2[:, 4 * r + 1, :], op=ALU.add), reads=['Sf', 'g2'], writes=['Sf'])
    P.op('dve', lambda e: e.tensor_copy(out=Tb[:, 3, :], in_=Hl[:, 1, NTL, :]), reads=['Hl'], writes=['Tb'])
    for r in range(3, 0, -1):
        P.op('dve', lambda e, r=r: e.tensor_tensor(out=Tb[:, r - 1, :], in0=Tb[:, r, :], in1=g2[:, 4 * r + 2, :], op=ALU.mult), reads=['Tb', 'g2'], writes=['Tb'])
        P.op('dve', lambda e, r=r: e.tensor_tensor(out=Tb[:, r - 1, :], in0=Tb[:, r - 1, :], in1=g2[:, 4 * r + 3, :], op=ALU.add), reads=['Tb', 'g2'], writes=['Tb'])
    P.op('dve', lambda e: e.memset(state[:], 0.0), writes=['state'])
    for r in range(4):
        P.op('dve', lambda e, r=r: e.scalar_tensor_tensor(out=state[:, 0, :], in0=Sf[:, r, :], scalar=selc[:, 12 + r:13 + r], in1=state[:, 0, :],
                                                          op0=ALU.mult, op1=ALU.add), reads=['Sf', 'selc', 'state'], writes=['state'])
        P.op('dve', lambda e, r=r: e.scalar_tensor_tensor(out=state[:, 1, :], in0=Tb[:, r, :], scalar=selc[:, 16 + r:17 + r], in1=state[:, 1, :],
                                                          op0=ALU.mult, op1=ALU.add), reads=['Tb', 'selc', 'state'], writes=['state'])
    P.barrier()

    def ab_load(d, ti, p):
        a_t, b_t = a_ts[p], b_ts[p]
        P.op('act', lambda e: e.dma_start(out=a_t[:, :, :], in_=K.ad[d][:, :, ti * TO:(ti + 1) * TO]),
             writes=[('a_t', p, c) for c in range(8)], dma=('ald', p))
        P.op('act', lambda e: e.dma_start(out=b_t[:, :, :], in_=K.bd[d][:, :, ti * TO:(ti + 1) * TO]),
             writes=[('b_t', p, c) for c in range(8)], dma=('bld', p))

    def scan_only(d, ti, p):
        a_t, b_t, hh = a_ts[p], b_ts[p], h_ts[p]
        for c in range(8):
            rk = [('a_t', p, c), ('b_t', p, c), 'state']
            if d == 0:
                P.op('dve', lambda e, c=c: e.tensor_tensor_scan(out=hh[:, c, :], data0=a_t[:, c, :], data1=b_t[:, c, :],
                                                                initial=state[:, d, c:c + 1], op0=ALU.mult, op1=ALU.add),
                     reads=rk, writes=[('hh', p, c)])
            else:
                P.op('dve', lambda e, c=c: e.tensor_tensor_scan(out=hh[:, c, :][:, ::-1], data0=a_t[:, c, :][:, ::-1], data1=b_t[:, c, :][:, ::-1],
                                                                initial=state[:, d, c:c + 1], op0=ALU.mult, op1=ALU.add),
                     reads=rk, writes=[('hh', p, c)])
        col = TO - 1 if d == 0 else 0
        P.op('dve', lambda e: e.tensor_copy(out=state[:, d, :], in_=hh[:, :, col]), reads=[('hh', p, c) for c in range(8)], writes=['state'])
        return hh

    ab_load(0, 0, 0)
    for ti in range(NTL):
        p = ti % 2
        hh = scan_only(0, ti, p)
        if ti + 1 < NTL:
            ab_load(0, ti + 1, 1 - p)
        P.op('sp', lambda e, ti=ti, hh=hh: e.dma_start(out=K.hfs[:, :, ti * TO:(ti + 1) * TO], in_=hh[:, :, :]),
             reads=[('hh', p, c) for c in range(8)], dma=('hfst', p))
    P.barrier()
    hfs_ = [sb(f"hf{i}", [128, 8, TO]) for i in range(2)]
    gls_ = [sb(f"gl{i}", [128, 8, TO], BF16) for i in range(2)]
    yTs_ = [sb(f"yT{i}", [128, 8, TO], BF16) for i in range(2)]
    G5 = sb("G5o", [128, D])
    xr = [sb(f"xro{s}", [128, D]) for s in range(2)]
    tmp = [sb(f"tmpo{s}", [128, D]) for s in range(2)]
    if K.nomod:
        P.op('dve', lambda e: e.memset(G5[:], 1.0), writes=['G5o'])
    else:
        P.op('sp', lambda e: e.dma_start(out=G5[:], in_=K.modrows[l, 0:1, 5 * D:6 * D].partition_broadcast(128)),
             reads=[('modrows', l)], writes=['G5o'], dma='G5o')
    on = 0
    def b_loads(ti):
        p = ti % 2
        P.op('act', lambda e: e.dma_start(out=hfs_[p][:], in_=K.hfs[:, :, ti * TO:(ti + 1) * TO]), writes=[('hf', p)], dma=('hf', p))
        P.op('act', lambda e: e.dma_start(out=gls_[p][:], in_=K.gg[:, :, ti * TO:(ti + 1) * TO]), writes=[('gl', p)], dma=('gl', p))
        ab_load(1, ti, p)

    b_loads(NTL - 1)
    for ti in range(NTL - 1, -1, -1):
        p = ti % 2
        hf, gl, yT = hfs_[p], gls_[p], yTs_[p]
        khf, kgl, kyT = ('hf', p), ('gl', p), ('yT', p)
        hh = scan_only(1, ti, p)
        for c in range(8):
            P.op('pool', lambda e, c=c, hh=hh, hf=hf: e.tensor_tensor(out=hf[:, c, :], in0=hf[:, c, :], in1=hh[:, c, :], op=ALU.add),
                 reads=[khf, ('hh', p, c)], writes=[khf])
            P.op('dve', lambda e, c=c, hf=hf, gl=gl, yT=yT: e.tensor_tensor(out=yT[:, c, :], in0=hf[:, c, :], in1=gl[:, c, :], op=ALU.mult),
                 reads=[khf, kgl], writes=[kyT])
        if ti - 1 >= 0:
            b_loads(ti - 1)
        for bi in range(TO // 128):
            so = on % 2
            on += 1
            r0 = HALO + ti * TO + bi * 128
            P.op('sp', lambda e, so=so, r0=r0: e.dma_start(out=xr[so][:], in_=src[r0:r0 + 128, :]), writes=[('xro', so)], dma=('xro', so))
            for fh in range(2):
                bki = fh + 2 * (on % 2)
                for kc in range(8):
                    P.op('pe', lambda e, bki=bki, kc=kc, bi=bi, fh=fh, yT=yT: e.matmul(
                        K.bank[bki][:, :], lhsT=yT[:, kc, bi * 128:(bi + 1) * 128], rhs=owout[:, kc, fh * 512:(fh + 1) * 512],
                        start=(kc == 0), stop=(kc == 7)), reads=[kyT, 'owout'], writes=[('bank', bki)], signal=(kc == 7))
                P.op('dve', lambda e, bki=bki, so=so, fh=fh: e.tensor_tensor(
                    out=tmp[so][:, fh * 512:(fh + 1) * 512], in0=K.bank[bki][:, :], in1=G5[:, fh * 512:(fh + 1) * 512], op=ALU.mult),
                    reads=[('bank', bki), 'G5o'], writes=[('tmpo', so, fh)])
                P.op('dve', lambda e, so=so, fh=fh: e.tensor_tensor(
                    out=tmp[so][:, fh * 512:(fh + 1) * 512], in0=tmp[so][:, fh * 512:(fh + 1) * 512], in1=xr[so][:, fh * 512:(fh + 1) * 512], op=ALU.add),
                    reads=[('tmpo', so, fh), ('xro', so)], writes=[('tmpo', so, fh)])
            P.op('sp', lambda e, so=so, r0=r0: e.dma_start(out=dst[r0:r0 + 128, :], in_=tmp[so][:]),
                 reads=[('tmpo', so, 0), ('tmpo', so, 1)], dma=('sto', so))
    P.barrier()
    st.close()
```

```python
import numpy as np
import ml_dtypes
from contextlib import ExitStack
import concourse.bass as bass
import concourse.mybir as mybir
from concourse.bass_utils import run_bass_kernel_spmd

F32 = mybir.dt.float32
BF16 = mybir.dt.bfloat16
AF = mybir.ActivationFunctionType
ALU = mybir.AluOpType
AX = mybir.AxisListType

D = 1024
DFF = 2816
NJ = DFF // 128
SEQ = 16384
TOK = 4096
HALO = 128
NTH = TOK + 2 * HALO
LCTX = 256
EPS = 1e-6


class Prog:
    ENG = ('sp', 'act', 'dve', 'pool', 'pe')

    def __init__(self, nc, same_engine_sync=True):
        self.nc = nc
        self.ops = {e: [] for e in self.ENG}
        self.esem = {}
        self.ecnt = {}
        self.seen = {}
        self.last_w = {}
        self.readers = {}
        self.dsem = {}
        self.dcnt = {}
        self.same = same_engine_sync
        self.nsem = 0
        self.epoch()

    def _newsem(self, name):
        self.nsem += 1
        return self.nc.alloc_semaphore(f"{name}_{self.nsem}")

    def epoch(self):
        for e in self.ENG:
            self.esem[e] = self._newsem("e" + e)
            self.ecnt[e] = 0

    def op(self, eng, fn, reads=(), writes=(), dma=None, signal=True, extra=(), inc=16, nodep=False):
        deps = list(extra)
        if not nodep:
            for k in reads:
                if k in self.last_w:
                    deps.append(self.last_w[k])
            for k in writes:
                if k in self.last_w:
                    deps.append(self.last_w[k])
                deps.extend(self.readers.get(k, ()))
        m = {}
        for (s, v, pe_) in deps:
            if pe_ == eng and (eng == 'pe' or not self.same):
                continue
            key = (eng, id(s))
            if self.seen.get(key, (None, 0))[1] >= v:
                continue
            self.seen[key] = (s, v)
            m[id(s)] = (s, v)
        waits = list(m.values())
        if dma is not None:
            if dma not in self.dsem:
                self.dsem[dma] = self._newsem("d")
                self.dcnt[dma] = 0
            self.dcnt[dma] += inc
            sig = (self.dsem[dma], inc)
            tok = (self.dsem[dma], self.dcnt[dma], 'dma')
        elif signal:
            self.ecnt[eng] += 1
            sig = (self.esem[eng], 1)
            tok = (self.esem[eng], self.ecnt[eng], eng)
        else:
            sig = None
            tok = (self.esem[eng], self.ecnt[eng] + 1, eng)
        self.ops[eng].append((waits, fn, sig))
        for k in writes:
            self.last_w[k] = tok
            self.readers[k] = []
        for k in reads:
            self.readers.setdefault(k, []).append(tok)
        return tok

    def barrier(self):
        toks = [(self.esem[e], self.ecnt[e], e) for e in self.ENG if self.ecnt[e] > 0]
        toks += [(self.dsem[k], self.dcnt[k], 'dma') for k in self.dsem
                 if not (isinstance(k, tuple) and k[0] == 'wcast')]
        for e in self.ENG:
            self.op(e, None, extra=[t for t in toks if t[2] != e or e != 'pe'], signal=False)

    def emit(self):
        nc = self.nc
        with nc.Block() as block:
            for e, deco in (('sp', block.sync), ('act', block.scalar), ('dve', block.vector),
                            ('pool', block.gpsimd), ('pe', block.tensor)):
                lst = self.ops[e]

                def body(engine, lst=lst):
                    for waits, fn, sig in lst:
                        for (s, v) in waits:
                            engine.wait_ge(s, v)
                        if fn is None:
                            continue
                        ins = fn(engine)
                        if sig is not None:
                            ins.then_inc(sig[0], sig[1])
                deco(body)


class Ctx:
    pass


NAMES = []


def build(stages=("ffn00", "mix0", "ffn01", "ffn10", "mix1", "ffn11"), nomod=False):
    nc = bass.Bass("TRN2", target_bir_lowering=False)
    P = Prog(nc)
    K = Ctx()
    K.nc, K.P = nc, P

    NAMES.clear()

    def din(name, shape, dt=F32):
        NAMES.append(name)
        return nc.dram_tensor(name, list(shape), dt, kind="ExternalInput").ap()

    def dscr(name, shape, dt=F32):
        return nc.dram_tensor(name, list(shape), dt, kind="Internal").ap()

    K.xh = din("xh", [NTH, D])
    K.ctx = din("ctxb", [LCTX, D])
    K.ccT = din("ccT", [128, 8, 2])
    K.ident = din("ident", [128, 128])
    K.nomod = nomod
    if not nomod:
        K.ada_w = [din(f"ada_w{l}", [D, 9 * D]) for l in range(2)]
        K.ada_b = din("ada_b", [2, 9 * D])
        K.norm_g = din("norm_g", [2, 3, D])
    K.ffn_groups = [(l, i) for l in range(2) for i in range(2) if f"ffn{l}{i}" in stages]
    K.wg = {g: din(f"wg{g[0]}{g[1]}", [D, DFF]) for g in K.ffn_groups}
    K.wu = {g: din(f"wu{g[0]}{g[1]}", [D, DFF]) for g in K.ffn_groups}
    K.wd = {g: din(f"wd{g[0]}{g[1]}", [DFF, D]) for g in K.ffn_groups}
    K.fng = din("final_norm_g", [D])
    if "mix0" in stages:
        K.ev_w_in = din("ev_w_in", [D, 1792])
        K.ev_w_out = din("ev_w_out", [D, D])
        K.gm_norm_g = din("gm_norm_g", [1, 512])
        K.gm_ws = din("gm_ws", [4, 128, 128])
        K.gm_bs = din("gm_bs", [1, 4, 128])
        K.attn_sink = din("attn_sink", [1, 8])
        K.ropeC = din("ropeC", [64, NBLK * 128])
        K.ropeS = din("ropeS", [64, NBLK * 128])
        K.amask = din("amask", [4, 128, 512])
        K.winb = dscr("winb", [128, 8, 1792], BF16)
        K.woutb = dscr("woutb", [128, 8, D], BF16)
    if "mix1" in stages:
        K.od_w_in = din("od_w_in", [D, 2048])
        K.od_w_out = din("od_w_out", [D, D])
        K.rg_wa = din("rg_wa", [2, 4, 256, 256])
        K.rg_wx = din("rg_wx", [2, 4, 256, 256])
        K.odcols = din("odcols", [128, 11, 8])
        K.selc = din("selc", [128, 24])
        K.owinb = dscr("owinb", [128, 8, 2048], BF16)
        K.owoutb = dscr("owoutb", [128, 8, D], BF16)
        K.wab = dscr("wab", [128, 16, 256], BF16)
        K.wxb = dscr("wxb", [128, 16, 256], BF16)
        K.gg = dscr("gg", [128, 8, TOK], BF16)
        K.xxs = dscr("xxs", [128, 8, TOK + 3])
        K.xxc = dscr("xxc", [128, 8, LCTX + 3])
        K.xcs = dscr("xcs", [128, 8, TOK])
        K.hfs = dscr("hfs", [128, 8, TOK])
        K.ad = [dscr(f"ad{d}", [128, 8, TOK]) for d in range(2)]
        K.bd = [dscr(f"bd{d}", [128, 8, TOK]) for d in range(2)]
        K.bnc1 = dscr("bnc1", [3, D])
        K.gath1 = dscr("gath1", [12, D])
        K.bnc2 = dscr("bnc2", [4, D])
        K.gath2 = dscr("gath2", [16, D])
    K.out = nc.dram_tensor("out", [TOK, D], F32, kind="ExternalOutput").ap()

    K.wgb = [[dscr(f"wgb{l}{i}", [NJ, 128, 8, 128], BF16) for i in range(2)] for l in range(2)]
    K.wub = [[dscr(f"wub{l}{i}", [NJ, 128, 8, 128], BF16) for i in range(2)] for l in range(2)]
    K.wdb = [[dscr(f"wdb{l}{i}", [DFF, D], BF16) for i in range(2)] for l in range(2)]
    K.modrows = dscr("modrows", [2, 2, 9 * D])
    K.xa = dscr("xa", [NTH, D])
    K.xb = dscr("xb", [NTH, D])
    K.ca = dscr("ca", [LCTX, D])
    K.cb = dscr("cb", [LCTX, D])

    es = ExitStack()
    K.es = es

    uid = [0]

    def sb(name, shape, dt=F32, stack=None):
        uid[0] += 1
        return (stack or es).enter_context(nc.sbuf_tensor(f"{name}_{uid[0]}", list(shape), dt))

    def ps(name, shape, dt=F32, stack=None):
        return (stack or es).enter_context(nc.psum_tensor(name, list(shape), dt))
    K.sb, K.ps = sb, ps

    K.identf = sb("identf", [128, 128])
    P.op('sp', lambda e: e.dma_start(out=K.identf[:], in_=K.ident[:, :]), writes=['identf'], dma='identf')
    K.epsc = sb("epsc", [128, 1])
    P.op('dve', lambda e: e.memset(K.epsc[:], EPS), writes=['epsc'])
    K.Acol = sb("Acol", [128, 2, 2, 3, 8])
    K.Bcol = sb("Bcol", [128, 2, 2, 3, 8])

    K.bank = [ps(f"bank{i}", [128, 512]) for i in range(8)]

    cast_weights(K)
    if nomod:
        P.op('dve', lambda e: e.memset(K.Acol[:], 1.0), writes=['Acol'])
        P.op('dve', lambda e: e.memset(K.Bcol[:], 0.0), writes=['Bcol'])
    else:
        prologue_mods(K)

    src_x, src_c = K.xh, K.ctx
    pp = [0]

    def nxt():
        pp[0] ^= 1
        return (K.xa, K.ca) if pp[0] else (K.xb, K.cb)
    if "ffn00" in stages:
        dx, dc = nxt()
        ffn_phase(K, 0, 0, src_x, dx, 0, NTH, src_c, dc)
        src_x, src_c = dx, dc
    if "mix0" in stages:
        dx, dc = nxt()
        even_mixer_phase(K, src_x, dx, src_c, dc)
        src_x, src_c = dx, dc
    if "ffn01" in stages:
        dx, dc = nxt()
        ffn_phase(K, 0, 1, src_x, dx, HALO, TOK, src_c, dc)
        src_x, src_c = dx, dc
    if "ffn10" in stages:
        dx, dc = nxt()
        ffn_phase(K, 1, 0, src_x, dx, HALO, TOK, src_c, dc)
        src_x, src_c = dx, dc
    if "mix1" in stages:
        dx, dc = nxt()
        odd_mixer_phase(K, src_x, dx, src_c)
        src_x, src_c = dx, dc
    if "ffn11" in stages:
        dx, dc = nxt()
        ffn_phase(K, 1, 1, src_x, dx, HALO, TOK, None, None, fuse_final=True)
        src_x, src_c = dx, dc
    else:
        final_phase(K, src_x)
    global LASTP, LASTNAMES
    LASTP = P
    LASTNAMES = set(NAMES)
    P.emit()
    return nc


def cast_weights(K):
    nc, P = K.nc, K.P
    def ffn_cast(l, i):
            if True:
                key = ('wcast', l, i)
                for (src, dst) in ((K.wg, K.wgb), (K.wu, K.wub)):
                    for j in range(NJ):
                        s_ap = src[(l, i)][:, j * 128:(j + 1) * 128].rearrange("(k p) m -> p k m", p=128)
                        d_ap = dst[l][i][j]
                        P.op('pool', lambda e, s_ap=s_ap, d_ap=d_ap: e.dma_start(out=d_ap, in_=s_ap),
                             writes=[key], dma=key, nodep=True)
                for r in range(8):
                    rs = DFF // 8
                    s_ap = K.wd[(l, i)][r * rs:(r + 1) * rs, :]
                    d_ap = K.wdb[l][i][r * rs:(r + 1) * rs, :]
                    P.op('pool', lambda e, s_ap=s_ap, d_ap=d_ap: e.dma_start(out=d_ap, in_=s_ap),
                         writes=[key], dma=key, nodep=True)


    done = set()
    for g in [(0, 0)]:
        if g in K.ffn_groups:
            ffn_cast(*g); done.add(g)
    if hasattr(K, "ev_w_in"):
        key = ('wcast', 'ev')
        for k in range(8):
            P.op('pool', lambda e, k=k: e.dma_start(out=K.winb[:, k, :], in_=K.ev_w_in[k * 128:(k + 1) * 128, :]),
                 writes=[key], dma=key, nodep=True)
            P.op('pool', lambda e, k=k: e.dma_start(out=K.woutb[:, k, :], in_=K.ev_w_out[k * 128:(k + 1) * 128, :]),
                 writes=[key], dma=key, nodep=True)
    if hasattr(K, "od_w_in"):
        key = ('wcast', 'od')
        for k in range(8):
            P.op('pool', lambda e, k=k: e.dma_start(out=K.owinb[:, k, :], in_=K.od_w_in[k * 128:(k + 1) * 128, :]),
                 writes=[key], dma=key, nodep=True)
            P.op('pool', lambda e, k=k: e.dma_start(out=K.owoutb[:, k, :], in_=K.od_w_out[k * 128:(k + 1) * 128, :]),
                 writes=[key], dma=key, nodep=True)
        for (srcw, dstw) in ((K.rg_wa, K.wab), (K.rg_wx, K.wxb)):
            P.op('pool', lambda e, srcw=srcw, dstw=dstw: e.dma_start(
                out=dstw[:, :, :], in_=srcw.rearrange("d h (kc p) o -> p (d h kc) o", p=128)),
                writes=[key], dma=key, nodep=True)

    for g in K.ffn_groups:
        if g not in done:
            ffn_cast(*g)


def prologue_mods(K):
    nc, P = K.nc, K.P
    st = ExitStack()
    sT = K.sb("sT", [128, 8, 2], stack=st)
    ccs = K.sb("ccs", [128, 8, 2], stack=st)
    ones = K.sb("ones", [128, 64], stack=st)
    sTrep = K.sb("sTrep", [128, 8, 128], BF16, stack=st)
    P.op('dve', lambda e: e.memset(ones[:], 1.0), writes=['ones'])
    P.op('sp', lambda e: e.dma_start(out=ccs[:], in_=K.ccT[:, :, :]), writes=['ccs'], dma='ccs')
    P.op('act', lambda e: e.activation(out=sT[:], in_=ccs[:], func=AF.Silu), reads=['ccs'], writes=['sT'])
    for k in range(8):
        for t in range(2):
            P.op('act', lambda e, k=k, t=t: e.activation(out=sTrep[:, k, t * 64:(t + 1) * 64], in_=ones[:], func=AF.Copy,
                                                         scale=sT[:, k, t:t + 1]),
                 reads=['sT', 'ones'], writes=['sTrep'])
    rows = K.sb("rows", [128, 9 * D], stack=st)
    brow = K.sb("brow", [128, 9 * D], stack=st)
    aw = [K.sb(f"aw{s}", [128, 3072], stack=st) for s in range(2)]
    awb = [K.sb(f"awb{s}", [128, 3072], BF16, stack=st) for s in range(2)]
    cols = K.sb("cols", [128, 2, 2, 9, 8], stack=st)
    ng = K.sb("ng", [128, 2, 3, 8], stack=st)
    for l in range(2):
        for i in range(3):
            P.op('sp', lambda e, l=l, i=i: e.dma_start(
                out=ng[:, l, i, :], in_=K.norm_g[l, i, :].rearrange("(k p) -> p k", p=128),
                allow_slow_non_contiguous=True), writes=['ng'], dma='ng', nodep=True)
    n = 0
    for l in range(2):
        P.op('sp', lambda e, l=l: e.dma_start(out=brow[:], in_=K.ada_b[l:l + 1, :].partition_broadcast(128)),
             writes=['brow'], dma='brow')
        for cg in range(3):
            for k in range(8):
                s = n % 2
                n += 1
                P.op('sp', lambda e, l=l, cg=cg, k=k, s=s: e.dma_start(
                    out=aw[s][:], in_=K.ada_w[l][k * 128:(k + 1) * 128, cg * 3072:(cg + 1) * 3072]),
                    writes=[('aw', s)], dma=('aw', s))
                if n % 2:
                    P.op('dve', lambda e, s=s: e.tensor_copy(out=awb[s][:], in_=aw[s][:]), reads=[('aw', s)], writes=[('awb', s)])
                else:
                    P.op('act', lambda e, s=s: e.activation(out=awb[s][:], in_=aw[s][:], func=AF.Copy), reads=[('aw', s)], writes=[('awb', s)])
                for b in range(6):
                    P.op('pe', lambda e, k=k, s=s, b=b: e.matmul(
                        K.bank[b][:, :], lhsT=sTrep[:, k, :], rhs=awb[s][:, b * 512:(b + 1) * 512],
                        start=(k == 0), stop=(k == 7)),
                        reads=['sTrep', ('awb', s)], writes=[('bank', b)], signal=(b == 5))
            for b in range(6):
                c0 = cg * 3072 + b * 512
                P.op('dve', lambda e, b=b, c0=c0: e.tensor_tensor(
                    out=rows[:, c0:c0 + 512], in0=K.bank[b][:, :], in1=brow[:, c0:c0 + 512], op=ALU.add),
                    reads=[('bank', b), 'brow'], writes=['rows'])
        P.op('sp', lambda e, l=l: e.dma_start(out=K.modrows[l, 0:1, :], in_=rows[0:1, :]),
             reads=['rows'], writes=[('modrows', l)], dma='modrows')
        P.op('sp', lambda e, l=l: e.dma_start(out=K.modrows[l, 1:2, :], in_=rows[64:65, :]),
             reads=['rows'], writes=[('modrows', l)], dma='modrows')
        tb = 0
        for m in (0, 1, 3, 4, 6, 7):
            for kh in range(2):
                bki = 6 + tb % 2
                tb += 1
                for kk in range(4):
                    c0 = m * D + (kh * 4 + kk) * 128
                    P.op('pe', lambda e, bki=bki, kk=kk, c0=c0: e.transpose(
                        out=K.bank[bki][:, kk * 128:(kk + 1) * 128], in_=rows[:, c0:c0 + 128], identity=K.identf[:]),
                        reads=['rows', 'identf'], writes=[('bank', bki)], signal=(kk == 3))
                bv = K.bank[bki][:, :].rearrange("p (a b) -> p a b", a=4)
                for t in range(2):
                    P.op('dve', lambda e, bv=bv, t=t, m=m, kh=kh, l=l: e.tensor_copy(
                        out=cols[:, l, t, m, kh * 4:(kh + 1) * 4], in_=bv[:, :, 64 * t]),
                        reads=[('bank', bki)], writes=['cols'])
    for l in range(2):
        for t in range(2):
            for i in range(3):
                P.op('dve', lambda e, l=l, t=t, i=i: e.scalar_tensor_tensor(
                    out=K.Acol[:, l, t, i, :], in0=cols[:, l, t, 3 * i + 1, :], scalar=1.0,
                    in1=ng[:, l, i, :], op0=ALU.add, op1=ALU.mult),
                    reads=['cols', 'ng'], writes=['Acol'])
                P.op('dve', lambda e, l=l, t=t, i=i: e.tensor_copy(
                    out=K.Bcol[:, l, t, i, :], in_=cols[:, l, t, 3 * i, :]),
                    reads=['cols'], writes=['Bcol'])
    P.barrier()
    st.close()


def norm_load(K, src, r0, nb, xs):
    P = K.P
    for bi in range(nb):
        P.op('sp', lambda e, bi=bi: e.dma_start(out=xs[:, bi, :], in_=src[r0 + bi * 128:r0 + (bi + 1) * 128, :]),
             writes=[('xs', bi)], dma=('xs', bi))


def norm_stats(K, nb, xs):
    P = K.P
    ss_t, rstd_t, junk_t = K.ss, K.rstd, K.junk
    for bi in range(nb):
        P.op('act', lambda e, bi=bi: e.activation(out=junk_t[:], in_=xs[:, bi, :], func=AF.Square,
                                                  accum_out=ss_t[:, bi:bi + 1]),
             reads=[('xs', bi)], writes=['junk', ('ss', bi)])
        P.op('act', lambda e, bi=bi: e.activation(out=ss_t[:, bi:bi + 1], in_=ss_t[:, bi:bi + 1], func=AF.Sqrt,
                                                  bias=K.epsc[:], scale=1.0 / D),
             reads=[('ss', bi), 'epsc'], writes=[('ss', bi)])
        P.op('dve', lambda e, bi=bi: e.reciprocal(out=rstd_t[:, bi:bi + 1], in_=ss_t[:, bi:bi + 1]),
             reads=[('ss', bi)], writes=[('rstd', bi)])
        P.op('act', lambda e, bi=bi: e.activation(out=xs[:, bi, :], in_=xs[:, bi, :], func=AF.Copy,
                                                  scale=rstd_t[:, bi:bi + 1]),
             reads=[('xs', bi), ('rstd', bi)], writes=[('xs', bi)])


def norm_T(K, nb, xs, l, t, i, xnT):
    P = K.P
    T = nb * 128
    nh = (T + 511) // 512
    for k in range(8):
        for h in range(nh):
            w = min(512, T - h * 512)
            bk = K.bank[(k * nh + h) % 2]
            bkey = ('bank', (k * nh + h) % 2)
            nbh = w // 128
            for bb in range(nbh):
                bi = h * 4 + bb
                P.op('pe', lambda e, bi=bi, bb=bb, k=k, bk=bk: e.transpose(
                    out=bk[:, bb * 128:(bb + 1) * 128], in_=xs[:, bi, k * 128:(k + 1) * 128], identity=K.identf[:]),
                    reads=[('xs', bi), 'identf'], writes=[bkey], signal=(bb == nbh - 1))
            P.op('act', lambda e, k=k, h=h, w=w, bk=bk: e.activation(
                out=xnT[:, k, h * 512:h * 512 + w], in_=bk[:, 0:w], func=AF.Identity,
                bias=K.Bcol[:, l, t, i, k:k + 1], scale=K.Acol[:, l, t, i, k:k + 1]),
                reads=[bkey, 'Acol', 'Bcol'], writes=['xnT'])


def norm_tile(K, tag, src, r0, nb, xs, l, t, i, xnT):
    norm_load(K, src, r0, nb, xs)
    norm_stats(K, nb, xs)
    norm_T(K, nb, xs, l, t, i, xnT)


def ffn_phase(K, l, i, src, dst, rstart, ntok, srcc, dstc, fuse_final=False):
    nc, P = K.nc, K.P
    st = ExitStack()
    tag = f"f{l}{i}"
    ni = 0 if i == 0 else 2
    TT = 1024
    wd_sb = K.sb("wd_sb", [128, NJ, D], BF16, stack=st)
    hT = K.sb("hT", [128, NJ, TT], BF16, stack=st)
    xnT = K.sb("xnT", [128, 8, TT], BF16, stack=st)
    xs = K.sb("xs", [128, TT // 128, D], stack=st)
    K.junk = K.sb("junk", [128, D], BF16, stack=st)
    K.ss = K.sb("ss", [128, 8], stack=st)
    K.rstd = K.sb("rstd", [128, 8], stack=st)
    wgu = [K.sb(f"wgu{s}", [128, 2, 8, 128], BF16, stack=st) for s in range(4)]
    sg = [K.sb(f"sg{s}", [128, 512], stack=st) for s in range(2)]
    G = K.sb("G", [128, 2, D], stack=st)
    xr = [K.sb(f"xr{s}", [128, D], stack=st) for s in range(2)]
    tmp = [K.sb(f"tmp{s}", [128, D], stack=st) for s in range(2)]

    wkey = ('wcast', l, i)
    P.op('sp', lambda e: e.dma_start(out=wd_sb[:], in_=K.wdb[l][i].rearrange("(j p) f -> p j f", p=128)),
         reads=[wkey], writes=['wd_sb'], dma='wd_sb')
    for t in range(2):
        if K.nomod:
            P.op('dve', lambda e, t=t: e.memset(G[:, t, :], 1.0), writes=['G'])
            continue
        P.op('sp', lambda e, t=t: e.dma_start(
            out=G[:, t, :], in_=K.modrows[l, t:t + 1, (2 + 6 * i) * D:(3 + 6 * i) * D].partition_broadcast(128)),
            reads=[('modrows', l)], writes=['G'], dma='G')

    tiles = []
    r = rstart
    while r < rstart + ntok:
        nbk = min(TT, rstart + ntok - r) // 128
        tiles.append((src, dst, r, nbk, 0))
        r += nbk * 128
    if srcc is not None:
        tiles.append((srcc, dstc, 0, LCTX // 128, 1))
    wn = 0
    on = 0
    pc = 0
    if fuse_final:
        P.op('sp', lambda e: e.dma_start(out=G[:, 1, :], in_=K.fng[None, :].partition_broadcast(128)), reads=['G'], writes=['G'], dma='G')
        fss = K.sb("fss2", [128, 2], stack=st)
        frs = K.sb("frs2", [128, 2], stack=st)
        junk_f = K.junk
    norm_load(K, tiles[0][0], tiles[0][2], tiles[0][3], xs)
    norm_stats(K, tiles[0][3], xs)
    norm_T(K, tiles[0][3], xs, l, tiles[0][4], ni, xnT)
    for ti, (s_ap, d_ap, r0, nb, t) in enumerate(tiles):
        T = nb * 128
        nxt_t = tiles[ti + 1] if ti + 1 < len(tiles) else None
        if nxt_t is not None:
            norm_load(K, nxt_t[0], nxt_t[2], nxt_t[3], xs)
        for j in range(NJ):
            if j == 6 and nxt_t is not None:
                norm_stats(K, nxt_t[3], xs)
            s = wn % 4
            wn += 1
            P.op('sp', lambda e, s=s, j=j: e.dma_start(out=wgu[s][:, 0], in_=K.wgb[l][i][j]),
                 reads=[wkey], writes=[('wgu', s, 0)], dma=('wgu', s, 0))
            P.op('sp', lambda e, s=s, j=j: e.dma_start(out=wgu[s][:, 1], in_=K.wub[l][i][j]),
                 reads=[wkey], writes=[('wgu', s, 1)], dma=('wgu', s, 1))
            for hf in range((T + 511) // 512):
                w = min(512, T - hf * 512)
                p2 = pc % 2
                pc += 1
                bg, bu = K.bank[2 + 2 * p2], K.bank[3 + 2 * p2]
                kg, ku = ('bank', 2 + 2 * p2), ('bank', 3 + 2 * p2)
                for (g_or_u, bk, kk) in ((0, bg, kg), (1, bu, ku)):
                    for k in range(8):
                        P.op('pe', lambda e, s=s, k=k, bk=bk, g_or_u=g_or_u, w=w, hf=hf: e.matmul(
                            bk[:, 0:w], lhsT=wgu[s][:, g_or_u, k, :], rhs=xnT[:, k, hf * 512:hf * 512 + w], start=(k == 0), stop=(k == 7)),
                            reads=[('wgu', s, g_or_u), 'xnT'], writes=[kk], signal=(k == 7))
                P.op('act', lambda e, bg=bg, p2=p2, w=w: e.activation(out=sg[p2][:, 0:w], in_=bg[:, 0:w], func=AF.Silu),
                     reads=[kg], writes=[('sg', p2)])
                P.op('dve', lambda e, bu=bu, p2=p2, j=j, w=w, hf=hf: e.tensor_tensor(
                    out=hT[:, j, hf * 512:hf * 512 + w], in0=bu[:, 0:w], in1=sg[p2][:, 0:w], op=ALU.mult),
                    reads=[ku, ('sg', p2)], writes=['hT'])
        for bi in range(nb):
            so = on % 2
            on += 1
            P.op('sp', lambda e, so=so, bi=bi, s_ap=s_ap, r0=r0: e.dma_start(
                out=xr[so][:], in_=s_ap[r0 + bi * 128:r0 + (bi + 1) * 128, :]),
                writes=[('xr', so)], dma=('xr', so))
            for fh in range(2):
                bk = K.bank[6 + fh]
                kk = ('bank', 6 + fh)
                for j in range(NJ):
                    P.op('pe', lambda e, bk=bk, j=j, bi=bi, fh=fh: e.matmul(
                        bk[:, :], lhsT=hT[:, j, bi * 128:(bi + 1) * 128], rhs=wd_sb[:, j, fh * 512:(fh + 1) * 512],
                        start=(j == 0), stop=(j == NJ - 1)),
                        reads=['hT', 'wd_sb'], writes=[kk], signal=(j == NJ - 1))
                P.op('dve', lambda e, bk=bk, so=so, fh=fh, t=t: e.tensor_tensor(
                    out=tmp[so][:, fh * 512:(fh + 1) * 512], in0=bk[:, :], in1=G[:, t, fh * 512:(fh + 1) * 512],
                    op=ALU.mult), reads=[kk, 'G'], writes=[('tmp', so, fh)])
                P.op('dve', lambda e, so=so, fh=fh: e.scalar_tensor_tensor(
                    out=tmp[so][:, fh * 512:(fh + 1) * 512], in0=tmp[so][:, fh * 512:(fh + 1) * 512], scalar=0.5,
                    in1=xr[so][:, fh * 512:(fh + 1) * 512], op0=ALU.mult, op1=ALU.add),
                    reads=[('tmp', so, fh), ('xr', so)], writes=[('tmp', so, fh)])
            if fuse_final:
                tk = [('tmp', so, 0), ('tmp', so, 1)]
                P.op('act', lambda e, so=so: e.activation(out=junk_f[:], in_=tmp[so][:], func=AF.Square, accum_out=fss[:, so:so + 1]),
                     reads=tk, writes=['junk', ('fss2', so)])
                P.op('act', lambda e, so=so: e.activation(out=fss[:, so:so + 1], in_=fss[:, so:so + 1], func=AF.Sqrt, bias=K.epsc[:], scale=1.0 / D),
                     reads=[('fss2', so), 'epsc'], writes=[('fss2', so)])
                P.op('dve', lambda e, so=so: e.reciprocal(out=frs[:, so:so + 1], in_=fss[:, so:so + 1]), reads=[('fss2', so)], writes=[('frs2', so)])
                P.op('dve', lambda e, so=so: e.scalar_tensor_tensor(out=tmp[so][:], in0=tmp[so][:], scalar=frs[:, so:so + 1], in1=G[:, 1, :],
                                                                    op0=ALU.mult, op1=ALU.mult),
                     reads=tk + [('frs2', so), 'G'], writes=tk)
                P.op('sp', lambda e, so=so, bi=bi, r0=r0: e.dma_start(
                    out=K.out[r0 - HALO + bi * 128:r0 - HALO + (bi + 1) * 128, :], in_=tmp[so][:]),
                    reads=[('tmp', so, 0), ('tmp', so, 1)], dma=('st', so))
            else:
                P.op('sp', lambda e, so=so, bi=bi, d_ap=d_ap, r0=r0: e.dma_start(
                    out=d_ap[r0 + bi * 128:r0 + (bi + 1) * 128, :], in_=tmp[so][:]),
                    reads=[('tmp', so, 0), ('tmp', so, 1)], dma=('st', so))
        if nxt_t is not None:
            norm_T(K, nxt_t[3], xs, l, nxt_t[4], ni, xnT)
    if fuse_final:
        P.op('sp', None, extra=[(P.dsem[('st', so_)], P.dcnt[('st', so_)], 'dma') for so_ in range(2)], signal=False)
    P.barrier()
    st.close()


def final_phase(K, src):
    nc, P = K.nc, K.P
    st = ExitStack()
    Gf = K.sb("Gf", [128, D], stack=st)
    P.op('sp', lambda e: e.dma_start(out=Gf[:], in_=K.fng[None, :].partition_broadcast(128)), writes=['Gf'], dma='Gf')
    xt = [K.sb(f"fx{s}", [128, D], stack=st) for s in range(3)]
    junk = K.sb("fjunk", [128, D], stack=st)
    ss = K.sb("fss", [128, 32], stack=st)
    rs = K.sb("frs", [128, 32], stack=st)
    for bi in range(TOK // 128):
        s = bi % 3
        P.op('sp', lambda e, s=s, bi=bi: e.dma_start(out=xt[s][:], in_=src[HALO + bi * 128:HALO + (bi + 1) * 128, :]),
             writes=[('fx', s)], dma=('fx', s))
        P.op('act', lambda e, s=s, bi=bi: e.activation(out=junk[:], in_=xt[s][:], func=AF.Square,
                                                       accum_out=ss[:, bi:bi + 1]),
             reads=[('fx', s)], writes=['fjunk', ('fss', bi)])
        P.op('act', lambda e, bi=bi: e.activation(out=ss[:, bi:bi + 1], in_=ss[:, bi:bi + 1], func=AF.Sqrt,
                                                  bias=K.epsc[:], scale=1.0 / D),
             reads=[('fss', bi), 'epsc'], writes=[('fss', bi)])
        P.op('dve', lambda e, bi=bi: e.reciprocal(out=rs[:, bi:bi + 1], in_=ss[:, bi:bi + 1]),
             reads=[('fss', bi)], writes=[('frs', bi)])
        P.op('dve', lambda e, s=s, bi=bi: e.scalar_tensor_tensor(
            out=xt[s][:], in0=xt[s][:], scalar=rs[:, bi:bi + 1], in1=Gf[:], op0=ALU.mult, op1=ALU.mult),
            reads=[('fx', s), ('frs', bi), 'Gf'], writes=[('fx', s)])
        P.op('sp', lambda e, s=s, bi=bi: e.dma_start(out=K.out[bi * 128:(bi + 1) * 128, :], in_=xt[s][:]),
             reads=[('fx', s)], writes=['out'], dma=('fo', s))
    P.op('sp', None, extra=[(P.dsem[('fo', s)], P.dcnt[('fo', s)], 'dma') for s in range(3) if ('fo', s) in P.dsem], signal=False)
    st.close()


def host_inputs(inp, names=None):
    x = np.asarray(inp["x"], np.float32)
    maps = []
    ident = np.eye(128, dtype=np.float32)
    f32 = lambda a: np.ascontiguousarray(np.asarray(a, np.float32))
    shared = {k: f32(inp[k]) for k in ("ada_b", "norm_g", "final_norm_g")}
    for l in range(2):
        shared[f"ada_w{l}"] = f32(np.asarray(inp["ada_w"])[l])
        for i in range(2):
            shared[f"wg{l}{i}"] = f32(np.asarray(inp["ffn_w_gate"])[l, i])
            shared[f"wu{l}{i}"] = f32(np.asarray(inp["ffn_w_up"])[l, i])
            shared[f"wd{l}{i}"] = f32(np.asarray(inp["ffn_w_down"])[l, i])
    for k in ("ev_w_in", "ev_w_out", "gm_ws"):
        shared[k] = f32(np.asarray(inp[k])[0])
    for k in ("gm_norm_g", "gm_bs", "attn_sink"):
        shared[k] = f32(inp[k])
    for k in ("od_w_in", "od_w_out", "rg_wa", "rg_wx"):
        shared[k] = f32(np.asarray(inp[k])[0])
    col = lambda v: np.asarray(v, np.float32).reshape(8, 128).T
    od = np.zeros((128, 11, 8), np.float32)
    for kk in range(4):
        od[:, kk] = col(np.asarray(inp["conv_w"])[0, kk])
    od[:, 4] = col(np.asarray(inp["conv_b"])[0])
    for dd in range(2):
        od[:, 5 + dd] = col(np.asarray(inp["rg_ba"])[0, dd])
        od[:, 7 + dd] = col(np.asarray(inp["rg_bx"])[0, dd])
        od[:, 9 + dd] = col(np.asarray(inp["rg_lambda"])[0, dd])
    shared["odcols"] = od
    ii = np.arange(128)
    mprev = np.tile((ii[:, None] >= ii[None, :]).astype(np.float32), (1, 4))
    mnext = np.tile((ii[:, None] <= ii[None, :]).astype(np.float32), (1, 4))
    inv = (10000.0 ** (-np.arange(16, dtype=np.float32) / 16)).astype(np.float32)
    for core in range(4 * x.shape[0]):
        b, q = core // 4, core % 4
        xh = np.zeros((NTH, D), np.float32)
        lo, hi = q * TOK - HALO, (q + 1) * TOK + HALO
        slo, shi = max(lo, 0), min(hi, SEQ)
        xh[slo - lo:shi - lo] = x[b, slo:shi]
        cc = np.stack([np.asarray(inp["c"], np.float32)[b], np.asarray(inp["c_ctx"], np.float32)], 0)
        ccT = np.ascontiguousarray(cc.reshape(2, 8, 128).transpose(2, 1, 0))
        m = {"xh": xh, "ctxb": np.ascontiguousarray(np.asarray(inp["ctx"], np.float32)[b]), "ccT": ccT,
             "ident": ident}
        pos = np.clip(q * TOK - HALO + np.arange(NTH), 0, SEQ - 1)
        ar = (pos // 64).astype(np.float32)[None, :] * inv[:, None]
        ac = (pos % 64).astype(np.float32)[None, :] * inv[:, None]
        C = np.ones((64, NBLK * 128), np.float32)
        Sn = np.zeros((64, NBLK * 128), np.float32)
        C[0:16, :NTH] = np.cos(ar); C[16:32, :NTH] = np.cos(ar); C[32:48, :NTH] = np.cos(ac); C[48:64, :NTH] = np.cos(ac)
        Sn[0:16, :NTH] = -np.sin(ar); Sn[16:32, :NTH] = np.sin(ar); Sn[32:48, :NTH] = -np.sin(ac); Sn[48:64, :NTH] = np.sin(ac)
        m["ropeC"], m["ropeS"] = C, Sn
        sel = np.zeros(24, np.float32)
        if q > 0:
            sel[3 * (q - 1) + 2] = 1.0
        if q < 3:
            sel[3 * (q + 1) + 0] = 1.0
            sel[3 * (q + 1) + 1] = 1.0
        sel[12 + q] = 1.0
        sel[16 + q] = 1.0
        m["selc"] = np.tile(sel[None, :], (128, 1))
        m["amask"] = np.stack([mprev, mnext, mprev * (q != 0), mnext * (q != 3)], 0).astype(np.float32)
        m.update(shared)
        if names is not None:
            m = {k: v for k, v in m.items() if k in names}
        maps.append(m)
    return maps


_NC_CACHE = {}


def kernel(**inputs):
    if "nc" not in _NC_CACHE:
        _NC_CACHE["nc"] = build()
    nc = _NC_CACHE["nc"]
    maps = host_inputs(inputs, LASTNAMES)
    res = run_bass_kernel_spmd(nc, maps, core_ids=list(range(8)))
    out = np.zeros((2, SEQ, D), np.float32)
    for core in range(8):
        b, q = core // 4, core % 4
        out[b, q * TOK:(q + 1) * TOK] = res.results[core]["out"]
    return out


NBLK = 36
TM = 256


def even_mixer_phase(K, src, dst, srcc, dstc):
    nc, P = K.nc, K.P
    st = ExitStack()
    sb = lambda n, s, d=F32: K.sb(n, s, d, stack=st)
    l = 0
    win = sb("win", [128, 8, 1792], BF16)
    wsw = sb("wsw", [128, 8, 640], BF16)
    wout = sb("wout", [128, 8, 1024], BF16)
    kT = sb("kT", [64, 2, NBLK * 128], BF16)
    vv = sb("vv", [128, NBLK, 2, 65], BF16)
    xs = sb("xs", [128, TM // 128, D])
    xnT = sb("xnT", [128, 8, TM], BF16)
    K.junk = sb("junk", [128, D], BF16)
    junk_m = K.junk
    K.ss = sb("ss", [128, 4])
    K.rstd = sb("rstd", [128, 4])
    ct = sb("ct", [64, TM])
    sn = sb("sn", [64, TM])
    t1 = sb("t1", [64, TM])
    t2 = sb("t2", [64, TM])
    identb = sb("identb", [128, 128], BF16)
    P.op('dve', lambda e: e.tensor_copy(out=identb[:], in_=K.identf[:]), reads=['identf'], writes=['identb'])
    wkey = ('wcast', 'ev')
    P.op('sp', lambda e: e.dma_start(out=win[:], in_=K.winb[:, :, :]), reads=[wkey], writes=['win'], dma='win')
    P.op('sp', lambda e: e.dma_start(out=wout[:], in_=K.woutb[:, :, :]), reads=[wkey], writes=['wout'], dma='wout')
    winv = win[:, :, 1024:1664].rearrange("p k (g b e) -> p k g b e", b=2, e=16)
    wswv = wsw[:].rearrange("p k (g b e) -> p k g b e", b=2, e=16)
    for k in range(8):
        for b in range(2):
            P.op('dve', lambda e, k=k, b=b: e.tensor_copy(out=wswv[:, k, :, b, :], in_=winv[:, k, :, 1 - b, :]),
                 reads=['win'], writes=['wsw'])
    P.op('dve', lambda e: e.memset(vv[:, :, :, 64:65], 1.0), writes=['vv1'])

    def kv_tile(s_ap, r0, nb, t, blk0):
        T = nb * 128
        norm_tile(K, None, s_ap, r0, nb, xs, l, t, 1, xnT)
        P.op('sp', lambda e: e.dma_start(out=ct[:, 0:T], in_=K.ropeC[:, blk0 * 128:blk0 * 128 + T]), writes=['ct'], dma='ct')
        P.op('sp', lambda e: e.dma_start(out=sn[:, 0:T], in_=K.ropeS[:, blk0 * 128:blk0 * 128 + T]), writes=['sn'], dma='sn')
        for kvh in range(2):
            for (w_t, c0, bki) in ((win, 1536 + kvh * 64, 2), (wsw, 512 + kvh * 64, 3)):
                for k in range(8):
                    P.op('pe', lambda e, w_t=w_t, c0=c0, bki=bki, k=k: e.matmul(
                        K.bank[bki][0:64, 0:T], lhsT=w_t[:, k, c0:c0 + 64], rhs=xnT[:, k, 0:T], start=(k == 0), stop=(k == 7)),
                        reads=['win', 'wsw', 'xnT'], writes=[('bank', bki)], signal=(k == 7))
            P.op('dve', lambda e: e.tensor_tensor(out=t1[:, 0:T], in0=K.bank[2][0:64, 0:T], in1=ct[:, 0:T], op=ALU.mult),
                 reads=[('bank', 2), 'ct'], writes=['t1'])
            P.op('dve', lambda e: e.tensor_tensor(out=t2[:, 0:T], in0=K.bank[3][0:64, 0:T], in1=sn[:, 0:T], op=ALU.mult),
                 reads=[('bank', 3), 'sn'], writes=['t2'])
            P.op('dve', lambda e, kvh=kvh: e.tensor_tensor(out=kT[:, kvh, blk0 * 128:blk0 * 128 + T], in0=t1[:, 0:T], in1=t2[:, 0:T], op=ALU.add),
                 reads=['t1', 't2'], writes=['kT'])
        for bi in range(nb):
            bki = 4 + bi % 2
            for k in range(8):
                P.op('pe', lambda e, bki=bki, k=k, bi=bi: e.matmul(
                    K.bank[bki][:, 0:128], lhsT=xnT[:, k, bi * 128:(bi + 1) * 128], rhs=win[:, k, 1664:1792],
                    start=(k == 0), stop=(k == 7)), reads=['win', 'xnT'], writes=[('bank', bki)], signal=(k == 7))
            P.op('act', lambda e, bki=bki, bi=bi: e.activation(
                out=vv[:, blk0 + bi, :, 0:64], in_=K.bank[bki][:, 0:128].rearrange("p (a b) -> p a b", a=2), func=AF.Copy),
                reads=[('bank', bki)], writes=['vv'])

    r = 0
    while r < NTH:
        nb = min(TM, NTH - r) // 128
        kv_tile(src, r, nb, 0, r // 128)
        r += nb * 128
    kv_tile(srcc, 0, 2, 1, 34)

    uT = sb("uT", [128, 4, TM], BF16)
    qT = sb("qT", [64, 8, TM], BF16)
    zg = [sb(f"zg{s}", [128, 512]) for s in range(2)]
    zn = sb("zn", [128, TM // 128, 512], BF16)
    zss = sb("zss", [128, 2])
    zr = sb("zr", [128, 2])
    mixT = sb("mixT", [128, 8, TM], BF16)
    pT = [sb(f"pT{s}", [128, 512], BF16) for s in range(10)]
    btok = sb("btok", [128, 512], BF16)
    den = sb("den", [128, 4])
    rec = sb("rec", [128, 4])
    masks = sb("masks", [128, 4, 512])
    bsB = sb("bsB", [128, 4, TM])
    gmg = sb("gmg", [128, 512])
    G5 = sb("G5", [128, 2, D])
    esink = sb("esink", [128, 8])
    wsT = sb("wsT", [128, 4, 128], BF16)
    wsl = sb("wsl", [128, 4, 128])
    tmpa = sb("tmpa", [128, TM])
    xr = [sb(f"xr{s}", [128, D]) for s in range(2)]
    tmp = [sb(f"tmp{s}", [128, D]) for s in range(2)]
    for m in range(4):
        P.op('sp', lambda e, m=m: e.dma_start(out=masks[:, m, :], in_=K.amask[m]), writes=['masks'], dma='masks')
    for rr in range(TM // 128):
        P.op('sp', lambda e, rr=rr: e.dma_start(out=bsB[:, :, rr * 128:(rr + 1) * 128],
                                               in_=K.gm_bs[0:1, :, :].partition_broadcast(128)), writes=['bsB'], dma='bsB')
    P.op('sp', lambda e: e.dma_start(out=gmg[:], in_=K.gm_norm_g[0:1, :].partition_broadcast(128)), writes=['gmg'], dma='gmg')
    P.op('sp', lambda e: e.dma_start(out=esink[:], in_=K.attn_sink[0:1, :].partition_broadcast(128)), writes=['esink'], dma='esink')
    P.op('act', lambda e: e.activation(out=esink[:], in_=esink[:], func=AF.Exp), reads=['esink'], writes=['esink'])
    for t in range(2):
        if K.nomod:
            P.op('dve', lambda e, t=t: e.memset(G5[:, t, :], 1.0), writes=['G5'])
        else:
            P.op('sp', lambda e, t=t: e.dma_start(out=G5[:, t, :], in_=K.modrows[l, t:t + 1, 5 * D:6 * D].partition_broadcast(128)),
                 reads=[('modrows', l)], writes=['G5'], dma='G5')
    P.op('sp', lambda e: e.dma_start(out=wsl[:], in_=K.gm_ws.rearrange("g p q -> p g q")), writes=['wsl'], dma='wsl')
    for g in range(4):
        P.op('pe', lambda e, g=g: e.transpose(out=K.bank[2][:, g * 128:(g + 1) * 128], in_=wsl[:, g, :], identity=K.identf[:]),
             reads=['wsl', 'identf'], writes=[('bank', 2)], signal=(g == 3))
    P.op('dve', lambda e: e.tensor_copy(out=wsT[:], in_=K.bank[2][:, :].rearrange("p (g q) -> p g q", g=4)),
         reads=[('bank', 2)], writes=['wsT'])
    bankT = K.bank[3][:, :].bitcast(BF16)
    pcnt = [0]
    on = [0]

    def mix_prefetch(s_ap, d_ap, r0, nb, t, blk0):
        T = nb * 128
        norm_load(K, s_ap, r0, nb, xs)
        P.op('sp', lambda e: e.dma_start(out=ct[:, 0:T], in_=K.ropeC[:, blk0 * 128:blk0 * 128 + T]), writes=['ct'], dma='ct')
        P.op('sp', lambda e: e.dma_start(out=sn[:, 0:T], in_=K.ropeS[:, blk0 * 128:blk0 * 128 + T]), writes=['sn'], dma='sn')

    def mix_tile(s_ap, d_ap, r0, nb, t, blk0, nxt=None):
        T = nb * 128
        norm_stats(K, nb, xs)
        norm_T(K, nb, xs, l, t, 1, xnT)
        for c in range(4):
            bki = 2 + c % 2
            for k in range(8):
                P.op('pe', lambda e, bki=bki, k=k, c=c: e.matmul(
                    K.bank[bki][:, 0:T], lhsT=win[:, k, c * 128:(c + 1) * 128], rhs=xnT[:, k, 0:T], start=(k == 0), stop=(k == 7)),
                    reads=['win', 'xnT'], writes=[('bank', bki)], signal=(k == 7))
            P.op('act', lambda e, bki=bki, c=c: e.activation(out=uT[:, c, 0:T], in_=K.bank[bki][:, 0:T], func=AF.Gelu_apprx_tanh),
                 reads=[('bank', bki)], writes=['uT'])
        for h in range(8):
            for (w_t, c0, bki) in ((win, 1024 + h * 64, 2), (wsw, h * 64, 3)):
                for k in range(8):
                    P.op('pe', lambda e, w_t=w_t, c0=c0, bki=bki, k=k: e.matmul(
                        K.bank[bki][0:64, 0:T], lhsT=w_t[:, k, c0:c0 + 64], rhs=xnT[:, k, 0:T], start=(k == 0), stop=(k == 7)),
                        reads=['win', 'wsw', 'xnT'], writes=[('bank', bki)], signal=(k == 7))
            P.op('dve', lambda e: e.tensor_tensor(out=t1[:, 0:T], in0=K.bank[2][0:64, 0:T], in1=ct[:, 0:T], op=ALU.mult),
                 reads=[('bank', 2), 'ct'], writes=['t1'])
            P.op('dve', lambda e: e.tensor_tensor(out=t2[:, 0:T], in0=K.bank[3][0:64, 0:T], in1=sn[:, 0:T], op=ALU.mult),
                 reads=[('bank', 3), 'sn'], writes=['t2'])
            P.op('dve', lambda e, h=h: e.tensor_tensor(out=qT[:, h, 0:T], in0=t1[:, 0:T], in1=t2[:, 0:T], op=ALU.add),
                 reads=['t1', 't2'], writes=['qT'])
        for bi in range(nb):
            bki = 4 + bi % 2
            for k in range(8):
                P.op('pe', lambda e, bki=bki, k=k, bi=bi: e.matmul(
                    K.bank[bki][:, :], lhsT=xnT[:, k, bi * 128:(bi + 1) * 128], rhs=win[:, k, 512:1024],
                    start=(k == 0), stop=(k == 7)), reads=['win', 'xnT'], writes=[('bank', bki)], signal=(k == 7))
            zs = bi % 2
            P.op('act', lambda e, bki=bki, zs=zs: e.activation(out=zg[zs][:], in_=K.bank[bki][:, :], func=AF.Gelu_apprx_tanh),
                 reads=[('bank', bki)], writes=[('zg', zs)])
            P.op('act', lambda e, zs=zs: e.activation(out=junk_m[:, 0:512], in_=zg[zs][:], func=AF.Square, accum_out=zss[:, zs:zs + 1]),
                 reads=[('zg', zs)], writes=['junk', ('zss', zs)])
            P.op('act', lambda e, zs=zs: e.activation(out=zss[:, zs:zs + 1], in_=zss[:, zs:zs + 1], func=AF.Sqrt, bias=K.epsc[:], scale=1.0 / 512),
                 reads=[('zss', zs), 'epsc'], writes=[('zss', zs)])
            P.op('dve', lambda e, zs=zs: e.reciprocal(out=zr[:, zs:zs + 1], in_=zss[:, zs:zs + 1]), reads=[('zss', zs)], writes=[('zr', zs)])
            P.op('dve', lambda e, zs=zs, bi=bi: e.scalar_tensor_tensor(out=zn[:, bi, :], in0=zg[zs][:], scalar=zr[:, zs:zs + 1], in1=gmg[:],
                                                                       op0=ALU.mult, op1=ALU.mult),
                 reads=[('zg', zs), ('zr', zs), 'gmg'], writes=['zn'])
        for g in range(4):
            bki = 2 + g % 2
            for bi in range(nb):
                P.op('pe', lambda e, bki=bki, g=g, bi=bi: e.matmul(
                    K.bank[bki][:, bi * 128:(bi + 1) * 128], lhsT=zn[:, bi, g * 128:(g + 1) * 128], rhs=wsT[:, g, :],
                    start=True, stop=True), reads=['zn', 'wsT'], writes=[('bank', bki)], signal=(bi == nb - 1))
            P.op('dve', lambda e, bki=bki, g=g: e.tensor_tensor(out=tmpa[:, 0:T], in0=K.bank[bki][:, 0:T], in1=bsB[:, g, 0:T], op=ALU.add),
                 reads=[('bank', bki), 'bsB'], writes=['tmpa'])
            P.op('dve', lambda e, g=g: e.tensor_tensor(out=mixT[:, g, 0:T], in0=tmpa[:, 0:T], in1=uT[:, g, 0:T], op=ALU.mult),
                 reads=['tmpa', 'uT'], writes=['mixT'])
        for bi in range(nb):
            blk = blk0 + bi
            if t == 0:
                kbs = [(blk - 1, 2 if blk == 1 else 0), (blk, None), (blk + 1, 3 if blk == TOK // 128 else 1), (34, None), (35, None)]
            else:
                kbs = [(34, None), (35, None)]
            for kvh in range(2):
                slots = []
                for (kb, mk) in kbs:
                    sl = pcnt[0] % 10
                    pcnt[0] += 1
                    slots.append(sl)
                    bki = 4 + sl % 2
                    P.op('pe', lambda e, bki=bki, kb=kb, kvh=kvh, bi=bi: e.matmul(
                        K.bank[bki][:, :], lhsT=kT[:, kvh, kb * 128:(kb + 1) * 128],
                        rhs=qT[:, 4 * kvh:4 * kvh + 4, bi * 128:(bi + 1) * 128], start=True, stop=True),
                        reads=['kT', 'qT'], writes=[('bank', bki)])
                    P.op('act', lambda e, bki=bki, sl=sl: e.activation(out=pT[sl][:], in_=K.bank[bki][:, :], func=AF.Exp, scale=0.125),
                         reads=[('bank', bki)], writes=[('pT', sl)])
                    if mk is not None:
                        P.op('dve', lambda e, sl=sl, mk=mk: e.tensor_tensor(out=pT[sl][:], in0=pT[sl][:], in1=masks[:, mk, :], op=ALU.mult),
                             reads=[('pT', sl), 'masks'], writes=[('pT', sl)])
                bko = 6 + kvh
                for h in range(4):
                    for ki, (kb, mk) in enumerate(kbs):
                        P.op('pe', lambda e, bko=bko, h=h, ki=ki, kb=kb, kvh=kvh, sl=slots[ki]: e.matmul(
                            K.bank[bko][:, h * 65:(h + 1) * 65], lhsT=pT[sl][:, h * 128:(h + 1) * 128], rhs=vv[:, kb, kvh, :],
                            start=(ki == 0), stop=(ki == len(kbs) - 1)),
                            reads=[('pT', slots[ki]), 'vv', 'vv1'], writes=[('bank', bko)], signal=(h == 3 and ki == len(kbs) - 1))
                ov = K.bank[bko][:, 0:260].rearrange("p (h c) -> p h c", c=65)
                P.op('dve', lambda e, ov=ov, kvh=kvh: e.tensor_tensor(out=den[:], in0=ov[:, :, 64], in1=esink[:, 4 * kvh:4 * kvh + 4], op=ALU.add),
                     reads=[('bank', bko), 'esink'], writes=['den'])
                P.op('dve', lambda e: e.reciprocal(out=rec[:], in_=den[:]), reads=['den'], writes=['rec'])
                for h in range(4):
                    P.op('act', lambda e, ov=ov, h=h, kvh=kvh: e.activation(
                        out=btok[:, (4 * kvh + h) * 64:(4 * kvh + h + 1) * 64], in_=ov[:, h, 0:64], func=AF.Copy, scale=rec[:, h:h + 1]),
                        reads=[('bank', bko), 'rec'], writes=['btok'])
            for c in range(4):
                P.op('pe', lambda e, c=c: e.transpose(out=bankT[:, c * 128:(c + 1) * 128], in_=btok[:, c * 128:(c + 1) * 128], identity=identb[:]),
                     reads=['btok', 'identb'], writes=[('bank', 3)], signal=(c == 3))
            P.op('act', lambda e, bi=bi: e.activation(out=mixT[:, 4:8, bi * 128:(bi + 1) * 128],
                                                      in_=bankT[:, 0:512].rearrange("p (c q) -> p c q", c=4), func=AF.Copy),
                 reads=[('bank', 3)], writes=['mixT'])
        if nxt is not None:
            mix_prefetch(*nxt)
        for bi in range(nb):
            so = on[0] % 2
            on[0] += 1
            P.op('sp', lambda e, so=so, bi=bi: e.dma_start(out=xr[so][:], in_=s_ap[r0 + bi * 128:r0 + (bi + 1) * 128, :]),
                 writes=[('xr', so)], dma=('xr', so))
            for fh in range(2):
                bki = fh
                for kc in range(8):
                    P.op('pe', lambda e, bki=bki, kc=kc, bi=bi, fh=fh: e.matmul(
                        K.bank[bki][:, :], lhsT=mixT[:, kc, bi * 128:(bi + 1) * 128], rhs=wout[:, kc, fh * 512:(fh + 1) * 512],
                        start=(kc == 0), stop=(kc == 7)), reads=['mixT', 'wout'], writes=[('bank', bki)], signal=(kc == 7))
                P.op('dve', lambda e, bki=bki, so=so, fh=fh: e.tensor_tensor(
                    out=tmp[so][:, fh * 512:(fh + 1) * 512], in0=K.bank[bki][:, :], in1=G5[:, t, fh * 512:(fh + 1) * 512], op=ALU.mult),
                    reads=[('bank', bki), 'G5'], writes=[('tmp', so, fh)])
                P.op('dve', lambda e, so=so, fh=fh: e.tensor_tensor(
                    out=tmp[so][:, fh * 512:(fh + 1) * 512], in0=tmp[so][:, fh * 512:(fh + 1) * 512], in1=xr[so][:, fh * 512:(fh + 1) * 512], op=ALU.add),
                    reads=[('tmp', so, fh), ('xr', so)], writes=[('tmp', so, fh)])
            P.op('sp', lambda e, so=so, bi=bi: e.dma_start(out=d_ap[r0 + bi * 128:r0 + (bi + 1) * 128, :], in_=tmp[so][:]),
                 reads=[('tmp', so, 0), ('tmp', so, 1)], dma=('st', so))

    mtiles = [(src, dst, HALO + s * TM, TM // 128, 0, (HALO + s * TM) // 128) for s in range(TOK // TM)]
    mtiles.append((srcc, dstc, 0, 2, 1, 34))
    mix_prefetch(*mtiles[0])
    for mi, mt in enumerate(mtiles):
        mix_tile(*mt, nxt=(mtiles[mi + 1] if mi + 1 < len(mtiles) else None))
    P.barrier()
    st.close()


TO = 256
RG = [[0, 1, 2, 3], [4, 5, 6, 7]]


def odd_mixer_phase(K, src, dst, srcc):
    nc, P = K.nc, K.P
    st = ExitStack()
    stA = ExitStack()
    sb = lambda n, s, d=F32: K.sb(n, s, d, stack=st)
    sbA = lambda n, s, d=F32: K.sb(n, s, d, stack=stA)
    l = 1
    NTL = TOK // TO
    TA = 512
    NTA = TOK // TA
    owout = sb("owout", [128, 8, 1024], BF16)
    wa = sb("wa", [128, 16, 256], BF16)
    wx = sb("wx", [128, 16, 256], BF16)
    oc_ = sb("odcols", [128, 11, 8])
    cl = sb("cl", [128, 2, 8])
    selc = sb("selc", [128, 24])
    owin = sbA("owin", [128, 8, 2048], BF16)
    xs = sbA("xs", [128, TA // 128, D])
    xnT = sbA("xnT", [128, 8, TA], BF16)
    K.junk = sbA("junk", [128, D], BF16)
    K.ss = sbA("ss", [128, 4])
    K.rstd = sbA("rstd", [128, 4])
    wkey = ('wcast', 'od')
    P.op('sp', lambda e: e.dma_start(out=owin[:], in_=K.owinb[:, :, :]), reads=[wkey], writes=['owin'], dma='owin')
    P.op('sp', lambda e: e.dma_start(out=owout[:], in_=K.owoutb[:, :, :]), reads=[wkey], writes=['owout'], dma='owout')
    P.op('sp', lambda e: e.dma_start(out=wa[:], in_=K.wab[:, :, :]), reads=[wkey], writes=['wa'], dma='wa')
    P.op('sp', lambda e: e.dma_start(out=wx[:], in_=K.wxb[:, :, :]), reads=[wkey], writes=['wx'], dma='wx')
    P.op('sp', lambda e: e.dma_start(out=oc_[:], in_=K.odcols[:, :, :]), writes=['odcols'], dma='odcols')
    P.op('act', lambda e: e.activation(out=cl[:], in_=oc_[:, 9:11, :], func=AF.Exp, scale=-1.0), reads=['odcols'], writes=['cl'])
    P.op('dve', lambda e: e.tensor_scalar(out=cl[:], in0=cl[:], scalar1=1.0, scalar2=None, op0=ALU.add), reads=['cl'], writes=['cl'])
    P.op('act', lambda e: e.activation(out=cl[:], in_=cl[:], func=AF.Ln), reads=['cl'], writes=['cl'])
    P.op('dve', lambda e: e.tensor_scalar(out=cl[:], in0=cl[:], scalar1=-8.0, scalar2=None, op0=ALU.mult), reads=['cl'], writes=['cl'])

    gT = sbA("gT", [128, 8, TA], BF16)
    xxT = sbA("xxT", [128, 8, TA])
    zero3 = sbA("zero3", [128, 8, 3])
    P.op('dve', lambda e: e.memset(zero3[:], 0.0), writes=['zero3'])
    P.op('sp', lambda e: e.dma_start(out=K.xxc[:, :, 0:1], in_=zero3[:, :, 0:1], allow_slow_non_contiguous=True), reads=['zero3'], dma='z0')
    P.op('sp', lambda e: e.dma_start(out=K.xxc[:, :, LCTX + 1:LCTX + 3], in_=zero3[:, :, 0:2]), reads=['zero3'], dma='z1')

    def proj_tile(s_ap, r0, nb, t, ti):
        T = nb * 128
        norm_tile(K, None, s_ap, r0, nb, xs, l, t, 1, xnT)
        for c in range(8):
            if t == 0:
                bki = 2 + c % 2
                for k in range(8):
                    P.op('pe', lambda e, bki=bki, k=k, c=c: e.matmul(
                        K.bank[bki][:, 0:T], lhsT=owin[:, k, c * 128:(c + 1) * 128], rhs=xnT[:, k, 0:T], start=(k == 0), stop=(k == 7)),
                        reads=['owin', 'xnT'], writes=[('bank', bki)], signal=(k == 7))
                P.op('act', lambda e, bki=bki, c=c: e.activation(out=gT[:, c, 0:T], in_=K.bank[bki][:, 0:T], func=AF.Gelu_apprx_tanh),
                     reads=[('bank', bki)], writes=['gT'])
            bki = 4 + c % 2
            for k in range(8):
                P.op('pe', lambda e, bki=bki, k=k, c=c: e.matmul(
                    K.bank[bki][:, 0:T], lhsT=owin[:, k, 1024 + c * 128:1024 + (c + 1) * 128], rhs=xnT[:, k, 0:T], start=(k == 0), stop=(k == 7)),
                    reads=['owin', 'xnT'], writes=[('bank', bki)], signal=(k == 7))
            P.op('dve', lambda e, bki=bki, c=c: e.tensor_copy(out=xxT[:, c, 0:T], in_=K.bank[bki][:, 0:T]),
                 reads=[('bank', bki)], writes=['xxT'])
        if t == 0:
            P.op('sp', lambda e: e.dma_start(out=K.gg[:, :, ti * TA:ti * TA + T], in_=gT[:, :, 0:T]), reads=['gT'], dma='ggst')
            P.op('sp', lambda e: e.dma_start(out=K.xxs[:, :, 1 + ti * TA:1 + ti * TA + T], in_=xxT[:, :, 0:T]), reads=['xxT'], dma='xxst')
            if ti == 0:
                for j in range(2):
                    P.op('sp', lambda e, j=j: e.dma_start(out=K.bnc1[j].rearrange("(c p) -> p c", p=128), in_=xxT[:, :, j],
                                                         allow_slow_non_contiguous=True), reads=['xxT'], writes=['bnc1'], dma='bnc1')
            if ti == NTA - 1:
                P.op('sp', lambda e: e.dma_start(out=K.bnc1[2].rearrange("(c p) -> p c", p=128), in_=xxT[:, :, T - 1],
                                                 allow_slow_non_contiguous=True), reads=['xxT'], writes=['bnc1'], dma='bnc1')
        else:
            P.op('sp', lambda e: e.dma_start(out=K.xxc[:, :, 1:1 + T], in_=xxT[:, :, 0:T]), reads=['xxT'], dma='xxst')

    for ti in range(NTA):
        proj_tile(src, HALO + ti * TA, TA // 128, 0, ti)
    proj_tile(srcc, 0, LCTX // 128, 1, 0)

    P.op('pool', lambda e: e.collective_compute("AllGather", ALU.bypass, replica_groups=RG, ins=[K.bnc1[:, :].opt()], outs=[K.gath1[:, :].opt()]),
         reads=['bnc1'], writes=['gath1'], dma='ag1', inc=1)
    g1 = sbA("g1", [128, 12, 8])
    for r in range(12):
        P.op('sp', lambda e, r=r: e.dma_start(out=g1[:, r, :], in_=K.gath1[r].rearrange("(c p) -> p c", p=128), allow_slow_non_contiguous=True),
             reads=['gath1'], writes=['g1'], dma='g1')
    P.op('sp', lambda e: e.dma_start(out=selc[:], in_=K.selc[:, :]), writes=['selc'], dma='selc')
    halo = sbA("halo", [128, 8, 3])
    P.op('dve', lambda e: e.memset(halo[:], 0.0), writes=['halo'])
    for j, rows in ((0, [2, 5, 8, 11]), (1, [0, 3, 6, 9]), (2, [1, 4, 7, 10])):
        for r in rows:
            P.op('dve', lambda e, j=j, r=r: e.scalar_tensor_tensor(out=halo[:, :, j], in0=g1[:, r, :], scalar=selc[:, r:r + 1], in1=halo[:, :, j],
                                                                  op0=ALU.mult, op1=ALU.add), reads=['g1', 'selc', 'halo'], writes=['halo'])
    P.op('sp', lambda e: e.dma_start(out=K.xxs[:, :, 0:1], in_=halo[:, :, 0:1], allow_slow_non_contiguous=True), reads=['halo', 'xxT'], dma='h0')
    P.op('sp', lambda e: e.dma_start(out=K.xxs[:, :, TOK + 1:TOK + 3], in_=halo[:, :, 1:3]), reads=['halo', 'xxT'], dma='h1')
    P.barrier()
    stA.close()

    xin = sb("xin", [128, 8, TO + 3])
    xc = sb("xc", [128, 8, TO])
    xcb = sb("xcb", [128, 8, TO], BF16)
    r_ts = [sb(f"r_t{i}", [128, 8, TO]) for i in range(2)]
    i_ts = [sb(f"i_t{i}", [128, 8, TO]) for i in range(2)]
    a_ts = [sb(f"a_t{i}", [128, 8, TO]) for i in range(2)]
    b_ts = [sb(f"b_t{i}", [128, 8, TO]) for i in range(2)]
    h_ts = [sb(f"h_t{i}", [128, 8, TO]) for i in range(2)]
    h_t = h_ts[0]
    rsum = sb("rsum", [128, 2, NTL + 1, 8])
    Hl = sb("Hl", [128, 2, NTL + 1, 8])
    Al = sb("Al", [128, 2, NTL + 1, 8])
    state = sb("state", [128, 2, 8])
    one1 = sb("one1", [128, 1])
    P.op('dve', lambda e: e.memset(one1[:], 1.0), writes=['one1'])
    P.op('dve', lambda e: e.memset(rsum[:], 0.0), writes=['rsum'])

    def conv_tile(xsrc, c0, T, store):
        P.op('sp', lambda e: e.dma_start(out=xin[:, :, 0:T + 3], in_=xsrc[:, :, c0:c0 + T + 3]), writes=['xin'], dma='xin')
        for c in range(8):
            P.op('dve', lambda e, c=c: e.tensor_scalar(out=xc[:, c, 0:T], in0=xin[:, c, 0:T], scalar1=oc_[:, 0, c:c + 1],
                                                       scalar2=oc_[:, 4, c:c + 1], op0=ALU.mult, op1=ALU.add),
                 reads=['xin', 'odcols'], writes=[('xc', c)])
            for k in range(1, 4):
                P.op('dve', lambda e, c=c, k=k: e.scalar_tensor_tensor(out=xc[:, c, 0:T], in0=xin[:, c, k:k + T], scalar=oc_[:, k, c:c + 1],
                                                                      in1=xc[:, c, 0:T], op0=ALU.mult, op1=ALU.add),
                     reads=['xin', 'odcols', ('xc', c)], writes=[('xc', c)])
            P.op('pool', lambda e, c=c: e.tensor_copy(out=xcb[:, c, 0:T], in_=xc[:, c, 0:T]), reads=[('xc', c)], writes=[('xcb', c)])
        if store is not None:
            P.op('sp', lambda e: e.dma_start(out=store, in_=xc[:, :, 0:T]), reads=[('xc', c) for c in range(8)], dma='xcst')

    def load_xc(xsrc, T):
        P.op('sp', lambda e: e.dma_start(out=xc[:, :, 0:T], in_=xsrc), writes=[('xc', c) for c in range(8)], dma='xcld')
        for c in range(8):
            P.op('act', lambda e, c=c: e.activation(out=xcb[:, c, 0:T], in_=xc[:, c, 0:T], func=AF.Copy), reads=[('xc', c)], writes=[('xcb', c)])

    def gates_scan(d, T, ti, init_state):
        a_t, b_t = a_ts[d], b_ts[d]
        r_t, i_t = r_ts[d], i_ts[d]
        for c in range(8):
            hd, oc = c // 2, c % 2
            for (wt, bki, dest, brow_i, acc) in ((wa, 2 + c % 2, r_t, 5 + d, True), (wx, 4 + c % 2, i_t, 7 + d, False)):
                for kc in range(2):
                    P.op('pe', lambda e, wt=wt, bki=bki, kc=kc, hd=hd, oc=oc: e.matmul(
                        K.bank[bki][:, 0:T], lhsT=wt[:, (d * 4 + hd) * 2 + kc, oc * 128:(oc + 1) * 128], rhs=xcb[:, 2 * hd + kc, 0:T],
                        start=(kc == 0), stop=(kc == 1)),
                        reads=['wa', 'wx', ('xcb', 2 * hd + kc)], writes=[('bank', bki)], signal=(kc == 1))
                if acc:
                    P.op('act', lambda e, bki=bki, dest=dest, c=c, brow_i=brow_i: e.activation(
                        out=dest[:, c, 0:T], in_=K.bank[bki][:, 0:T], func=AF.Sigmoid, bias=oc_[:, brow_i, c:c + 1],
                        accum_out=rsum[:, d, ti, c:c + 1]),
                        reads=[('bank', bki), 'odcols'], writes=[('r_t', d, c), 'rsum'])
                else:
                    P.op('act', lambda e, bki=bki, dest=dest, c=c, brow_i=brow_i: e.activation(
                        out=dest[:, c, 0:T], in_=K.bank[bki][:, 0:T], func=AF.Sigmoid, bias=oc_[:, brow_i, c:c + 1]),
                        reads=[('bank', bki), 'odcols'], writes=[('i_t', d, c)])
        for c in range(8):
            P.op('act', lambda e, c=c: e.activation(out=a_t[:, c, 0:T], in_=r_t[:, c, 0:T], func=AF.Exp, scale=cl[:, d, c:c + 1]),
                 reads=[('r_t', d, c), 'cl'], writes=[('a_t', d, c)])
            P.op('pool', lambda e, c=c: e.tensor_tensor(out=r_t[:, c, 0:T], in0=a_t[:, c, 0:T], in1=a_t[:, c, 0:T], op=ALU.mult),
                 reads=[('a_t', d, c)], writes=[('r_t', d, c)])
        for c in range(8):
            P.op('act', lambda e, c=c: e.activation(out=r_t[:, c, 0:T], in_=r_t[:, c, 0:T], func=AF.Sqrt, bias=one1[:], scale=-1.0),
                 reads=[('r_t', d, c), 'one1'], writes=[('r_t', d, c)])
            P.op('pool', lambda e, c=c: e.tensor_tensor(out=i_t[:, c, 0:T], in0=i_t[:, c, 0:T], in1=r_t[:, c, 0:T], op=ALU.mult),
                 reads=[('i_t', d, c), ('r_t', d, c)], writes=[('i_t', d, c)])
            P.op('pool', lambda e, c=c: e.tensor_tensor(out=b_t[:, c, 0:T], in0=i_t[:, c, 0:T], in1=xc[:, c, 0:T], op=ALU.mult),
                 reads=[('i_t', d, c), ('xc', c)], writes=[('b_t', d, c)])
            init = 0.0 if init_state is None else init_state[:, d, c:c + 1]
            rk = [('a_t', d, c), ('b_t', d, c)] + ([] if init_state is None else ['state'])
            if d == 0:
                P.op('dve', lambda e, c=c, init=init: e.tensor_tensor_scan(out=h_t[:, c, 0:T], data0=a_t[:, c, 0:T], data1=b_t[:, c, 0:T],
                                                                           initial=init, op0=ALU.mult, op1=ALU.add),
                     reads=rk, writes=[('h_t', c)])
            else:
                P.op('dve', lambda e, c=c, init=init: e.tensor_tensor_scan(out=h_t[:, c, T - 1::-1] if False else h_t[:, c, 0:T][:, ::-1],
                                                                           data0=a_t[:, c, 0:T][:, ::-1], data1=b_t[:, c, 0:T][:, ::-1],
                                                                           initial=init, op0=ALU.mult, op1=ALU.add),
                     reads=rk, writes=[('h_t', c)])

    hkeys = [('h_t', c) for c in range(8)]
    for ti in range(NTL + 1):
        ctxt = ti == NTL
        T = LCTX if ctxt else TO
        if ctxt:
            conv_tile(K.xxc, 0, T, None)
        else:
            conv_tile(K.xxs, ti * TO, T, None)
        for d in range(2):
            gates_scan(d, T, ti, None)
            if not ctxt:
                P.op('sp', lambda e, d=d, ti=ti: e.dma_start(out=K.ad[d][:, :, ti * TO:(ti + 1) * TO], in_=a_ts[d][:, :, :]),
                     reads=[('a_t', d, c) for c in range(8)], dma=('ast', d))
                P.op('sp', lambda e, d=d, ti=ti: e.dma_start(out=K.bd[d][:, :, ti * TO:(ti + 1) * TO], in_=b_ts[d][:, :, :]),
                     reads=[('b_t', d, c) for c in range(8)], dma=('bst', d))
            col = T - 1 if d == 0 else 0
            P.op('dve', lambda e, d=d, ti=ti, col=col: e.tensor_copy(out=Hl[:, d, ti, :], in_=h_t[:, :, col]), reads=hkeys, writes=['Hl'])
    for d in range(2):
        for ti in range(NTL + 1):
            P.op('dve', lambda e, d=d, ti=ti: e.tensor_tensor(out=Al[:, d, ti, :], in0=rsum[:, d, ti, :], in1=cl[:, d, :], op=ALU.mult),
                 reads=['rsum', 'cl'], writes=['Al'])
    P.op('act', lambda e: e.activation(out=Al[:], in_=Al[:], func=AF.Exp), reads=['Al'], writes=['Al'])
    summ = sb("summ", [128, 4, 8])
    P.op('dve', lambda e: e.memset(summ[:], 0.0), writes=['summ'])
    P.op('dve', lambda e: e.memset(summ[:, 0, :], 1.0), reads=['summ'], writes=['summ'])
    P.op('dve', lambda e: e.memset(summ[:, 2, :], 1.0), reads=['summ'], writes=['summ'])
    for ti in range(NTL):
        P.op('dve', lambda e, ti=ti: e.tensor_tensor(out=summ[:, 1, :], in0=summ[:, 1, :], in1=Al[:, 0, ti, :], op=ALU.mult), reads=['summ', 'Al'], writes=['summ'])
        P.op('dve', lambda e, ti=ti: e.tensor_tensor(out=summ[:, 1, :], in0=summ[:, 1, :], in1=Hl[:, 0, ti, :], op=ALU.add), reads=['summ', 'Hl'], writes=['summ'])
        P.op('dve', lambda e, ti=ti: e.tensor_tensor(out=summ[:, 0, :], in0=summ[:, 0, :], in1=Al[:, 0, ti, :], op=ALU.mult), reads=['summ', 'Al'], writes=['summ'])
    for ti in range(NTL - 1, -1, -1):
        P.op('dve', lambda e, ti=ti: e.tensor_tensor(out=summ[:, 3, :], in0=summ[:, 3, :], in1=Al[:, 1, ti, :], op=ALU.mult), reads=['summ', 'Al'], writes=['summ'])
        P.op('dve', lambda e, ti=ti: e.tensor_tensor(out=summ[:, 3, :], in0=summ[:, 3, :], in1=Hl[:, 1, ti, :], op=ALU.add), reads=['summ', 'Hl'], writes=['summ'])
        P.op('dve', lambda e, ti=ti: e.tensor_tensor(out=summ[:, 2, :], in0=summ[:, 2, :], in1=Al[:, 1, ti, :], op=ALU.mult), reads=['summ', 'Al'], writes=['summ'])
    for qi in range(4):
        P.op('sp', lambda e, qi=qi: e.dma_start(out=K.bnc2[qi].rearrange("(c p) -> p c", p=128), in_=summ[:, qi, :], allow_slow_non_contiguous=True),
             reads=['summ'], writes=['bnc2'], dma='bnc2')
    P.op('pool', lambda e: e.collective_compute("AllGather", ALU.bypass, replica_groups=RG, ins=[K.bnc2[:, :].opt()], outs=[K.gath2[:, :].opt()]),
         reads=['bnc2'], writes=['gath2'], dma='ag2', inc=1)
    g2 = sb("g2", [128, 16, 8])
    for r in range(16):
        P.op('sp', lambda e, r=r: e.dma_start(out=g2[:, r, :], in_=K.gath2[r].rearrange("(c p) -> p c", p=128), allow_slow_non_contiguous=True),
             reads=['gath2'], writes=['g2'], dma='g2')
    Sf = sb("Sf", [128, 4, 8])
    Tb = sb("Tb", [128, 4, 8])
    P.op('dve', lambda e: e.tensor_copy(out=Sf[:, 0, :], in_=Hl[:, 0, NTL, :]), reads=['Hl'], writes=['Sf'])
    for r in range(3):
        P.op('dve', lambda e, r=r: e.tensor_tensor(out=Sf[:, r + 1, :], in0=Sf[:, r, :], in1=g2[:, 4 * r + 0, :], op=ALU.mult), reads=['Sf', 'g2'], writes=['Sf'])
        P.op('dve', lambda e, r=r: e.tensor_tensor(out=Sf[:, r + 1, :], in0=Sf[:, r + 1, :], in1=g2[:, 4 * r + 1, :], op=ALU.add), reads=['Sf', 'g2'], writes=['Sf'])
    P.op('dve', lambda e: e.tensor_copy(out=Tb[:, 3, :], in_=Hl[:, 1, NTL, :]), reads=['Hl'], writes=['Tb'])
    for r in range(3, 0, -1):
        P.op('dve', lambda e, r=r: e.tensor_tensor(out=Tb[:, r - 1, :], in0=Tb[:, r, :], in1=g2[:, 4 * r + 2, :], op=ALU.mult), reads=['Tb', 'g2'], writes=['Tb'])
        P.op('dve', lambda e, r=r: e.tensor_tensor(out=Tb[:, r - 1, :], in0=Tb[:, r - 1, :], in1=g2[:, 4 * r + 3, :], op=ALU.add), reads=['Tb', 'g2'], writes=['Tb'])
    P.op('dve', lambda e: e.memset(state[:], 0.0), writes=['state'])
    for r in range(4):
        P.op('dve', lambda e, r=r: e.scalar_tensor_tensor(out=state[:, 0, :], in0=Sf[:, r, :], scalar=selc[:, 12 + r:13 + r], in1=state[:, 0, :],
                                                          op0=ALU.mult, op1=ALU.add), reads=['Sf', 'selc', 'state'], writes=['state'])
        P.op('dve', lambda e, r=r: e.scalar_tensor_tensor(out=state[:, 1, :], in0=Tb[:, r, :], scalar=selc[:, 16 + r:17 + r], in1=state[:, 1, :],
                                                          op0=ALU.mult, op1=ALU.add), reads=['Tb', 'selc', 'state'], writes=['state'])
    P.barrier()

    def ab_load(d, ti, p):
        a_t, b_t = a_ts[p], b_ts[p]
        P.op('sp', lambda e: e.dma_start(out=a_t[:, :, :], in_=K.ad[d][:, :, ti * TO:(ti + 1) * TO]),
             writes=[('a_t', p, c) for c in range(8)], dma=('ald', p))
        P.op('sp', lambda e: e.dma_start(out=b_t[:, :, :], in_=K.bd[d][:, :, ti * TO:(ti + 1) * TO]),
             writes=[('b_t', p, c) for c in range(8)], dma=('bld', p))

    def scan_only(d, ti, p):
        a_t, b_t, hh = a_ts[p], b_ts[p], h_ts[p]
        for c in range(8):
            rk = [('a_t', p, c), ('b_t', p, c), 'state']
            if d == 0:
                P.op('dve', lambda e, c=c: e.tensor_tensor_scan(out=hh[:, c, :], data0=a_t[:, c, :], data1=b_t[:, c, :],
                                                                initial=state[:, d, c:c + 1], op0=ALU.mult, op1=ALU.add),
                     reads=rk, writes=[('hh', p, c)])
            else:
                P.op('dve', lambda e, c=c: e.tensor_tensor_scan(out=hh[:, c, :][:, ::-1], data0=a_t[:, c, :][:, ::-1], data1=b_t[:, c, :][:, ::-1],
                                                                initial=state[:, d, c:c + 1], op0=ALU.mult, op1=ALU.add),
                     reads=rk, writes=[('hh', p, c)])
        col = TO - 1 if d == 0 else 0
        P.op('dve', lambda e: e.tensor_copy(out=state[:, d, :], in_=hh[:, :, col]), reads=[('hh', p, c) for c in range(8)], writes=['state'])
        return hh

    ab_load(0, 0, 0)
    for ti in range(NTL):
        p = ti % 2
        hh = scan_only(0, ti, p)
        if ti + 1 < NTL:
            ab_load(0, ti + 1, 1 - p)
        P.op('sp', lambda e, ti=ti, hh=hh: e.dma_start(out=K.hfs[:, :, ti * TO:(ti + 1) * TO], in_=hh[:, :, :]),
             reads=[('hh', p, c) for c in range(8)], dma=('hfst', p))
    P.barrier()
    hfs_ = [sb(f"hf{i}", [128, 8, TO]) for i in range(2)]
    gls_ = [sb(f"gl{i}", [128, 8, TO], BF16) for i in range(2)]
    yTs_ = [sb(f"yT{i}", [128, 8, TO], BF16) for i in range(2)]
    G5 = sb("G5o", [128, D])
    xr = [sb(f"xro{s}", [128, D]) for s in range(2)]
    tmp = [sb(f"tmpo{s}", [128, D]) for s in range(2)]
    if K.nomod:
        P.op('dve', lambda e: e.memset(G5[:], 1.0), writes=['G5o'])
    else:
        P.op('sp', lambda e: e.dma_start(out=G5[:], in_=K.modrows[l, 0:1, 5 * D:6 * D].partition_broadcast(128)),
             reads=[('modrows', l)], writes=['G5o'], dma='G5o')
    on = 0
    def b_loads(ti):
        p = ti % 2
        P.op('sp', lambda e: e.dma_start(out=hfs_[p][:], in_=K.hfs[:, :, ti * TO:(ti + 1) * TO]), writes=[('hf', p)], dma=('hf', p))
        P.op('sp', lambda e: e.dma_start(out=gls_[p][:], in_=K.gg[:, :, ti * TO:(ti + 1) * TO]), writes=[('gl', p)], dma=('gl', p))
        ab_load(1, ti, p)

    b_loads(NTL - 1)
    for ti in range(NTL - 1, -1, -1):
        p = ti % 2
        hf, gl, yT = hfs_[p], gls_[p], yTs_[p]
        khf, kgl, kyT = ('hf', p), ('gl', p), ('yT', p)
        hh = scan_only(1, ti, p)
        for c in range(8):
            P.op('pool', lambda e, c=c, hh=hh, hf=hf: e.tensor_tensor(out=hf[:, c, :], in0=hf[:, c, :], in1=hh[:, c, :], op=ALU.add),
                 reads=[khf, ('hh', p, c)], writes=[khf])
            P.op('dve', lambda e, c=c, hf=hf, gl=gl, yT=yT: e.tensor_tensor(out=yT[:, c, :], in0=hf[:, c, :], in1=gl[:, c, :], op=ALU.mult),
                 reads=[khf, kgl], writes=[kyT])
        if ti - 1 >= 0:
            b_loads(ti - 1)
        for bi in range(TO // 128):
            so = on % 2
            on += 1
            r0 = HALO + ti * TO + bi * 128
            P.op('sp', lambda e, so=so, r0=r0: e.dma_start(out=xr[so][:], in_=src[r0:r0 + 128, :]), writes=[('xro', so)], dma=('xro', so))
            for fh in range(2):
                bki = fh + 2 * (on % 2)
                for kc in range(8):
                    P.op('pe', lambda e, bki=bki, kc=kc, bi=bi, fh=fh, yT=yT: e.matmul(
                        K.bank[bki][:, :], lhsT=yT[:, kc, bi * 128:(bi + 1) * 128], rhs=owout[:, kc, fh * 512:(fh + 1) * 512],
                        start=(kc == 0), stop=(kc == 7)), reads=[kyT, 'owout'], writes=[('bank', bki)], signal=(kc == 7))
                P.op('dve', lambda e, bki=bki, so=so, fh=fh: e.tensor_tensor(
                    out=tmp[so][:, fh * 512:(fh + 1) * 512], in0=K.bank[bki][:, :], in1=G5[:, fh * 512:(fh + 1) * 512], op=ALU.mult),
                    reads=[('bank', bki), 'G5o'], writes=[('tmpo', so, fh)])
                P.op('dve', lambda e, so=so, fh=fh: e.tensor_tensor(
                    out=tmp[so][:, fh * 512:(fh + 1) * 512], in0=tmp[so][:, fh * 512:(fh + 1) * 512], in1=xr[so][:, fh * 512:(fh + 1) * 512], op=ALU.add),
                    reads=[('tmpo', so, fh), ('xro', so)], writes=[('tmpo', so, fh)])
            P.op('sp', lambda e, so=so, r0=r0: e.dma_start(out=dst[r0:r0 + 128, :], in_=tmp[so][:]),
                 reads=[('tmpo', so, 0), ('tmpo', so, 1)], dma=('sto', so))
    P.barrier()
    st.close()
```
